# Optimizing a Trainium2 kernel written in Bass

```python
import math
import jax
import jax.numpy as jnp
from jax import lax
import numpy as np

D_MODEL = 1024
BATCH = 16
SEQ = 2048
DEPTH = 2

HEAD_DIM = 64
D_MIX = D_MODEL
POOL_WINDOWS = (2, 4, 8, 16)
POOL_GROUPS = len(POOL_WINDOWS)
POOL_WIDTH = D_MIX // 4
POOL_GROUP_DIM = POOL_WIDTH // POOL_GROUPS
DIFF_WIDTH = D_MIX // 2
DIFF_V_DIM = 2 * HEAD_DIM
DIFF_HEADS = DIFF_WIDTH // DIFF_V_DIM
MOBA_WIDTH = D_MIX - POOL_WIDTH - DIFF_WIDTH
MOBA_HEADS = MOBA_WIDTH // HEAD_DIM
MOBA_BLOCK = 256
MOBA_TOPK = 3
MOBA_Q_CHUNK = 16
ATTN_Q_BLOCK = 128
IN_COLS = POOL_WIDTH + 3 * DIFF_WIDTH + 3 * MOBA_WIDTH
SPLITS = (POOL_WIDTH,
          POOL_WIDTH + DIFF_WIDTH,
          POOL_WIDTH + 2 * DIFF_WIDTH,
          POOL_WIDTH + 3 * DIFF_WIDTH,
          POOL_WIDTH + 3 * DIFF_WIDTH + MOBA_WIDTH,
          POOL_WIDTH + 3 * DIFF_WIDTH + 2 * MOBA_WIDTH)
D_FF = 2816
ROPE_THETA = 10000.0
NORM_EPS = 1e-6
NEG_INF = -1e30

kernel_name = "hybrid_pool_diffattn_moba_macaron"


def rmsnorm(x, g):
    xf = x.astype(jnp.float32)
    y = xf * lax.rsqrt(jnp.mean(xf * xf, axis=-1, keepdims=True) + NORM_EPS)
    return (y * g.astype(jnp.float32)).astype(x.dtype)


def swiglu_ffn(h, w_in, w_out):
    a, b = jnp.split(h @ w_in, 2, axis=-1)
    return (jax.nn.silu(a) * b) @ w_out


def rope_tables(seq):
    inv = ROPE_THETA ** (-jnp.arange(0, HEAD_DIM, 2, dtype=jnp.float32) / HEAD_DIM)
    ang = jnp.arange(seq, dtype=jnp.float32)[:, None] * inv[None, :]
    return jnp.cos(ang), jnp.sin(ang)


def rope(x, cos, sin):
    half = HEAD_DIM // 2
    xf = x.astype(jnp.float32)
    x1, x2 = xf[..., :half], xf[..., half:]
    out = jnp.concatenate([x1 * cos - x2 * sin, x2 * cos + x1 * sin], axis=-1)
    return out.astype(x.dtype)


def pool_mixer(u, pool_w, pool_scale):
    B, S, _ = u.shape
    uf = u.astype(jnp.float32)
    csum = jnp.pad(jnp.cumsum(uf, axis=1), ((0, 0), (1, 0), (0, 0)))
    t = jnp.arange(S)
    outs = []
    for g, w in enumerate(POOL_WINDOWS):
        c = csum[:, :, g * POOL_GROUP_DIM:(g + 1) * POOL_GROUP_DIM]
        start = jnp.maximum(t + 1 - w, 0)
        count = jnp.minimum(t + 1, w).astype(jnp.float32)
        outs.append((c[:, 1:] - c[:, start]) / count[None, :, None])
    pooled = (jnp.concatenate(outs, axis=-1) - uf).astype(u.dtype)
    pooled = pooled.reshape(B, S, POOL_GROUPS, POOL_GROUP_DIM)
    mixed = jnp.einsum('bsgc,gcd->bsgd', pooled, pool_w).reshape(B, S, POOL_WIDTH)
    return mixed * pool_scale


def diff_attention(q, k, v, lam, subln_g, lambda_init, cos, sin):
    B, S = q.shape[:2]
    q = rope(q.transpose(0, 2, 3, 1, 4), cos, sin)
    k = rope(k.transpose(0, 2, 3, 1, 4), cos, sin)
    v = v.transpose(0, 2, 1, 3)
    lf = lam.astype(jnp.float32)
    lam_val = jnp.exp(jnp.sum(lf[0] * lf[1])) - jnp.exp(jnp.sum(lf[2] * lf[3])) + lambda_init
    scale = HEAD_DIM ** -0.5
    nblk = S // ATTN_Q_BLOCK
    qb = q.reshape(B, DIFF_HEADS, 2, nblk, ATTN_Q_BLOCK, HEAD_DIM).transpose(3, 0, 1, 2, 4, 5)
    kpos = jnp.arange(S)

    def one_block(args):
        qblk, i = args
        qpos = i * ATTN_Q_BLOCK + jnp.arange(ATTN_Q_BLOCK)
        s = jnp.einsum('bhmqd,bhmkd->bhmqk', qblk, k).astype(jnp.float32) * scale
        s = jnp.where(kpos[None, :] <= qpos[:, None], s, NEG_INF)
        p = jax.nn.softmax(s, axis=-1)
        pdiff = p[:, :, 0] - lam_val * p[:, :, 1]
        return jnp.einsum('bhqk,bhkd->bhqd', pdiff.astype(v.dtype), v)

    o = lax.map(one_block, (qb, jnp.arange(nblk)))
    o = o.transpose(1, 2, 0, 3, 4).reshape(B, DIFF_HEADS, S, DIFF_V_DIM)
    o = rmsnorm(o, subln_g) * (1.0 - lambda_init)
    return o.transpose(0, 2, 1, 3).reshape(B, S, DIFF_WIDTH)


def moba_attention(q, k, v, cos, sin):
    B, S = q.shape[:2]
    H = MOBA_HEADS
    q = rope(q.transpose(0, 2, 1, 3), cos, sin)
    k = rope(k.transpose(0, 2, 1, 3), cos, sin)
    v = v.transpose(0, 2, 1, 3)
    nb = -(-S // MOBA_BLOCK)
    pad = nb * MOBA_BLOCK - S
    kblk = jnp.pad(k, ((0, 0), (0, 0), (0, pad), (0, 0))).reshape(B, H, nb, MOBA_BLOCK, HEAD_DIM)
    vblk = jnp.pad(v, ((0, 0), (0, 0), (0, pad), (0, 0))).reshape(B, H, nb, MOBA_BLOCK, HEAD_DIM)
    kmean = jnp.mean(kblk.astype(jnp.float32), axis=3)
    n_sel = min(MOBA_TOPK, nb - 1)
    scale = HEAD_DIM ** -0.5
    nq = S // MOBA_Q_CHUNK
    qc = q.reshape(B, H, nq, MOBA_Q_CHUNK, HEAD_DIM).transpose(2, 0, 1, 3, 4)
    bi = jnp.arange(B)[:, None, None, None]
    hi = jnp.arange(H)[None, :, None, None]
    key_off = jnp.arange(MOBA_BLOCK)
    blk_ids = jnp.arange(nb)

    def one_chunk(args):
        qblk, i = args
        qpos = i * MOBA_Q_CHUNK + jnp.arange(MOBA_Q_CHUNK)
        own = qpos // MOBA_BLOCK
        own_idx = jnp.broadcast_to(own[None, None, :, None], (B, H, MOBA_Q_CHUNK, 1))
        causal = (own[:, None] * MOBA_BLOCK + key_off[None, :]) <= qpos[:, None]
        own_valid = jnp.broadcast_to(causal[None, None, :, None, :], (B, H, MOBA_Q_CHUNK, 1, MOBA_BLOCK))
        if n_sel > 0:
            gate = jnp.einsum('bhqd,bhnd->bhqn', qblk.astype(jnp.float32), kmean)
            gate = jnp.where(blk_ids[None, :] < own[:, None], gate, NEG_INF)
            _, sel = lax.top_k(gate, n_sel)
            sel_valid = sel < own[None, None, :, None]
            idx = jnp.concatenate([sel.astype(own_idx.dtype), own_idx], axis=-1)
            valid = jnp.concatenate(
                [jnp.broadcast_to(sel_valid[..., None], (B, H, MOBA_Q_CHUNK, n_sel, MOBA_BLOCK)), own_valid],
                axis=3)
        else:
            idx = own_idx
            valid = own_valid
        kg = kblk[bi, hi, idx]
        vg = vblk[bi, hi, idx]
        s = jnp.einsum('bhqd,bhqnkd->bhqnk', qblk, kg).astype(jnp.float32) * scale
        s = jnp.where(valid, s, NEG_INF)
        p = jax.nn.softmax(s.reshape(B, H, MOBA_Q_CHUNK, -1), axis=-1).reshape(s.shape)
        return jnp.einsum('bhqnk,bhqnkd->bhqd', p.astype(vg.dtype), vg)

    o = lax.map(one_chunk, (qc, jnp.arange(nq)))
    o = o.transpose(1, 2, 0, 3, 4).reshape(B, H, S, HEAD_DIM)
    return o.transpose(0, 2, 1, 3).reshape(B, S, MOBA_WIDTH)


def token_mixing(h, w_in, w_out, pool_w, pool_scale, diff_lambda, diff_subln, lambda_init, cos, sin):
    B, S, _ = h.shape
    proj = h @ w_in
    u, dq, dk, dv, mq, mk, mv = jnp.split(proj, SPLITS, axis=-1)
    ya = pool_mixer(u, pool_w, pool_scale)
    yb = diff_attention(dq.reshape(B, S, DIFF_HEADS, 2, HEAD_DIM),
                        dk.reshape(B, S, DIFF_HEADS, 2, HEAD_DIM),
                        dv.reshape(B, S, DIFF_HEADS, DIFF_V_DIM),
                        diff_lambda, diff_subln, lambda_init, cos, sin)
    yc = moba_attention(mq.reshape(B, S, MOBA_HEADS, HEAD_DIM),
                        mk.reshape(B, S, MOBA_HEADS, HEAD_DIM),
                        mv.reshape(B, S, MOBA_HEADS, HEAD_DIM), cos, sin)
    return jnp.concatenate([ya, yb, yc], axis=-1) @ w_out


def setup_inputs(seed: int = 0) -> dict:
    key = jax.random.key(seed)
    ks = jax.random.split(key, 16)
    f32 = jnp.float32

    def nrm(k, shape, scale):
        return jax.random.normal(k, shape, f32) * scale

    def gain(k, shape):
        return 1.0 + 0.05 * jax.random.normal(k, shape, f32)

    return {
        "x": nrm(ks[0], (BATCH, SEQ, D_MODEL), 1.0),
        "ffn1_norm": gain(ks[1], (DEPTH, D_MODEL)),
        "ffn1_w_in": nrm(ks[2], (DEPTH, D_MODEL, 2 * D_FF), D_MODEL ** -0.5),
        "ffn1_w_out": nrm(ks[3], (DEPTH, D_FF, D_MODEL), D_FF ** -0.5),
        "mix_norm": gain(ks[4], (DEPTH, D_MODEL)),
        "mix_w_in": nrm(ks[5], (DEPTH, D_MODEL, IN_COLS), D_MODEL ** -0.5),
        "mix_w_out": nrm(ks[6], (DEPTH, D_MIX, D_MODEL), D_MIX ** -0.5),
        "pool_w": nrm(ks[7], (DEPTH, POOL_GROUPS, POOL_GROUP_DIM, POOL_GROUP_DIM), POOL_GROUP_DIM ** -0.5),
        "pool_scale": gain(ks[8], (DEPTH, POOL_WIDTH)),
        "diff_lambda": nrm(ks[9], (DEPTH, 4, HEAD_DIM), 0.1),
        "diff_subln": gain(ks[10], (DEPTH, DIFF_V_DIM)),
        "ffn2_norm": gain(ks[11], (DEPTH, D_MODEL)),
        "ffn2_w_in": nrm(ks[12], (DEPTH, D_MODEL, 2 * D_FF), D_MODEL ** -0.5),
        "ffn2_w_out": nrm(ks[13], (DEPTH, D_FF, D_MODEL), D_FF ** -0.5),
        "final_norm": gain(ks[14], (D_MODEL,)),
    }


def reference(x, ffn1_norm, ffn1_w_in, ffn1_w_out, mix_norm, mix_w_in, mix_w_out, pool_w, pool_scale,
              diff_lambda, diff_subln, ffn2_norm, ffn2_w_in, ffn2_w_out, final_norm):
    cos, sin = rope_tables(x.shape[1])
    for l in range(DEPTH):
        lambda_init = 0.8 - 0.6 * math.exp(-0.3 * l)
        x = x + 0.5 * swiglu_ffn(rmsnorm(x, ffn1_norm[l]), ffn1_w_in[l], ffn1_w_out[l])
        x = x + token_mixing(rmsnorm(x, mix_norm[l]), mix_w_in[l], mix_w_out[l], pool_w[l], pool_scale[l],
                             diff_lambda[l], diff_subln[l], lambda_init, cos, sin)
        x = x + 0.5 * swiglu_ffn(rmsnorm(x, ffn2_norm[l]), ffn2_w_in[l], ffn2_w_out[l])
    return rmsnorm(x, final_norm)
```

```python
import math
import numpy as np
import concourse.bass as bass
import concourse.mybir as mybir
from concourse.bass_utils import run_bass_kernel_spmd

F32 = mybir.dt.float32
BF16 = mybir.dt.bfloat16
ALU = mybir.AluOpType
AF = mybir.ActivationFunctionType
AX = mybir.AxisListType

D_MODEL = 1024
SEQ = 2048
DEPTH = 2
D_FF = 2816
NFF = D_FF // 128
IN_COLS = 2560
POOL_WINDOWS = (2, 4, 8, 16)
EPS = 1e-6
NEG = -30000.0
ENGS = ("pe", "act", "dve", "pool", "sp")
EPOCH = 8192


class Op:
    __slots__ = ("eng", "fn", "deps", "signal", "sem", "count", "is_dma", "idx")

    def __init__(self, eng, fn, deps, is_dma=False, sem=None):
        self.eng = eng
        self.fn = fn
        self.deps = deps
        self.signal = False
        self.sem = sem
        self.count = None
        self.is_dma = is_dma


class Buf:
    __slots__ = ("ap", "w", "r")

    def __init__(self, ap=None):
        self.ap = ap
        self.w = {}
        self.r = {}


class Prog:
    def __init__(self, nc):
        self.nc = nc
        self.q = {e: [] for e in ENGS}
        self.dma_sem_counts = {}

    def _mk(self, eng, fn, reads, writes, deps, is_dma, sem):
        d = [x for x in deps if x is not None]
        for b in reads:
            d.extend(b.w.values())
        for b in writes:
            d.extend(b.w.values())
            d.extend(b.r.values())
        if eng == "pe" and not is_dma:
            d = [x for x in d if x.is_dma or x.eng != "pe"]
        seen = set()
        dd = []
        for x in d:
            if id(x) not in seen:
                seen.add(id(x))
                dd.append(x)
        o = Op(eng, fn, dd, is_dma=is_dma, sem=sem)
        for x in dd:
            x.signal = True
        key = id(o) if is_dma else eng
        for b in reads:
            b.r[key] = o
        for b in writes:
            b.w = {key: o}
            b.r = {}
        self.q[eng].append(o)
        return o

    def op(self, eng, fn, reads=(), writes=(), deps=()):
        return self._mk(eng, fn, reads, writes, deps, False, None)

    def dma(self, eng, fn, sem, reads=(), writes=(), deps=()):
        o = self._mk(eng, fn, reads, writes, deps, True, sem)
        o.signal = True
        c = self.dma_sem_counts.get(id(sem), 0) + 16
        self.dma_sem_counts[id(sem)] = c
        o.count = c
        return o

    def emit(self, block, esems):
        for e in ENGS:
            c = 0
            for o in self.q[e]:
                if o.is_dma:
                    continue
                if o.signal:
                    o.sem = esems[e][c // EPOCH]
                    o.count = c % EPOCH + 1
                    c += 1

        def run(e, engobj):
            waited = {}
            for o in self.q[e]:
                for d in o.deps:
                    k = id(d.sem)
                    if waited.get(k, 0) >= d.count:
                        continue
                    engobj.wait_ge(d.sem, d.count)
                    waited[k] = d.count
                ins = o.fn(engobj)
                if o.signal:
                    ins.then_inc(o.sem, 16 if o.is_dma else 1)

        @block.tensor
        def _(eng):
            run("pe", eng)

        @block.scalar
        def _(eng):
            run("act", eng)

        @block.vector
        def _(eng):
            run("dve", eng)

        @block.gpsimd
        def _(eng):
            run("pool", eng)

        @block.sync
        def _(eng):
            run("sp", eng)


class Ring:
    def __init__(self, items):
        self.items = items
        self.i = 0

    def next(self):
        it = self.items[self.i % len(self.items)]
        self.i += 1
        return it


def _const_tables():
    inv = 10000.0 ** (-np.arange(0, 64, 2, dtype=np.float32) / 64.0)
    ang = np.arange(SEQ, dtype=np.float32)[:, None] * inv[None, :]
    cos = np.cos(ang).astype(np.float32)
    sin = np.sin(ang).astype(np.float32)
    cs = np.zeros((128, 16, 64), np.float32)
    cs[:, :, 0:32] = cos.reshape(16, 128, 32).transpose(1, 0, 2)
    cs[:, :, 32:64] = sin.reshape(16, 128, 32).transpose(1, 0, 2)
    mats = np.zeros((128, 15, 128), np.float32)
    mats[:, 0, :] = np.eye(128, dtype=np.float32)
    mats[:, 1, :] = 1.0
    p = np.arange(128)[:, None]
    f = np.arange(128)[None, :]
    mats[:, 2, :] = (p <= f).astype(np.float32)
    for wi, w in enumerate(POOL_WINDOWS):
        tp, t = p, f
        inwin = ((tp <= t) & (tp > t - w)).astype(np.float32)
        eye = (tp == t).astype(np.float32)
        mats[:, 3 + wi, :] = inwin / w - eye
        mats[:, 7 + wi, :] = ((tp - 128) > (t - w)).astype(np.float32) / w
        cnt = np.minimum(t + 1, w).astype(np.float32)
        mats[:, 11 + wi, :] = inwin / cnt - eye
    oh = np.zeros((8, 8, 128), np.float32)
    for j in range(8):
        oh[j, j, :] = 1.0
    q = np.arange(SEQ)
    own = q // 256
    negc = np.where(np.arange(8)[:, None] <= own[None, :], 0.0, NEG).astype(np.float32)
    nminit = np.zeros((128, 8, 8), np.float32)
    for o in range(8):
        nminit[:, o, o + 1:] = NEG
    return cs.reshape(128, 1024), mats, oh.reshape(8, 1024), negc, nminit.reshape(128, 64)


def _mix_perm():
    perm = list(range(0, 256))
    for h in range(4):
        perm += list(range(256 + h * 128, 256 + (h + 1) * 128))
        perm += list(range(768 + h * 128, 768 + (h + 1) * 128))
        perm += list(range(1280 + h * 128, 1280 + (h + 1) * 128))
    for p_ in range(2):
        perm += list(range(1792 + p_ * 128, 1792 + (p_ + 1) * 128))
        perm += list(range(2048 + p_ * 128, 2048 + (p_ + 1) * 128))
        perm += list(range(2304 + p_ * 128, 2304 + (p_ + 1) * 128))
    return np.array(perm)


NSMALL = 576
SP_NORM = 0
SP_PSC = 56
SP_SUB = 60
SP_LAM = 64


def _small_params(inp, depth):
    sp = np.zeros((128, NSMALL), np.float32)
    names = ["ffn1_norm", "mix_norm", "ffn2_norm"]
    for l in range(depth):
        for k, nm in enumerate(names):
            sp[:, SP_NORM + (l * 3 + k) * 8: SP_NORM + (l * 3 + k + 1) * 8] = inp[nm][l].reshape(8, 128).T
        sp[:, SP_PSC + l * 2: SP_PSC + l * 2 + 2] = inp["pool_scale"][l].reshape(2, 128).T
        sp[:, SP_SUB + l] = inp["diff_subln"][l]
        sp[:, SP_LAM + l * 256: SP_LAM + (l + 1) * 256] = inp["diff_lambda"][l].reshape(1, 256)
    sp[:, SP_NORM + 48: SP_NORM + 56] = inp["final_norm"].reshape(8, 128).T
    return sp


def _pool_bd(pool_w, depth):
    bd = np.zeros((depth, 2, 128, 128), np.float32)
    for l in range(depth):
        for g in range(4):
            cc, gh = g // 2, g % 2
            bd[l, cc, gh * 64:(gh + 1) * 64, gh * 64:(gh + 1) * 64] = pool_w[l, g]
    return bd


def build_program(nseq=2, depth=DEPTH, groups=(4, 4, 4, 4, 3, 3), stop=None, parts=("P", "D", "M"), dlevel=9, skip_ffn=False, ntb=16, nd=4):
    assert sum(groups) == NFF
    SMAX = max(groups)
    nc = bass.Bass("TRN2", target_bir_lowering=False)
    dr = lambda name, shape: nc.dram_tensor(name, list(shape), F32, kind="ExternalInput").ap()
    xT_d = dr("xT", (nseq, D_MODEL, SEQ))
    w_in_d = [dr("ffn1_w_in", (depth, D_MODEL, 2 * D_FF)), dr("ffn2_w_in", (depth, D_MODEL, 2 * D_FF))]
    w_out_d = [dr("ffn1_w_out", (depth, D_FF, D_MODEL)), dr("ffn2_w_out", (depth, D_FF, D_MODEL))]
    mixw_d = dr("mixw", (depth, D_MODEL, IN_COLS))
    mixo_d = dr("mixo", (depth, D_MODEL, D_MODEL))
    poolbd_d = dr("poolbd", (depth, 2, 128, 128))
    small_d = dr("smallp", (128, NSMALL))
    cs_d = dr("c_cs", (128, 1024))
    mats_d = dr("c_mats", (128, 15, 128))
    oh_d = dr("c_oh", (8, 1024))
    neg_d = dr("c_neg", (8, SEQ))
    nmi_d = dr("c_nmi", (128, 64))
    out_d = nc.dram_tensor("outT", [nseq, D_MODEL, SEQ], F32, kind="ExternalOutput").ap()

    U_WORDS = 16384
    TOTAL = 16384 + 8192 + NSMALL + 1024 + 960 + 512 + 2048 + 320 + 2048 + 1536 + 128 + 2048 + U_WORDS
    nsem_e = {"pe": 6, "act": 4, "dve": 6, "pool": 4, "sp": 1}
    NDMA = 30

    from contextlib import ExitStack
    with ExitStack() as es:
        arena = es.enter_context(nc.sbuf_tensor("arena", [128, TOTAL], F32))
        ps = es.enter_context(nc.psum_tensor("ps", [128, 8, 512], F32))
        esems = {e: [es.enter_context(nc.semaphore(f"s_{e}{i}")) for i in range(n)] for e, n in nsem_e.items()}
        dsems = [es.enter_context(nc.semaphore(f"d{i}")) for i in range(NDMA)]
        block = es.enter_context(nc.Block())
        dsem_i = [0]

        def newsem():
            s = dsems[dsem_i[0]]
            dsem_i[0] += 1
            return s

        P = Prog(nc)
        off = [0]

        def carve(nwords, dt=F32):
            v = arena[:, off[0]:off[0] + nwords]
            off[0] += nwords
            if dt is not F32:
                v = v.bitcast(dt)
            return v

        xT = carve(16384).rearrange("p (c t) -> p c t", c=8)
        xb = [[Buf() for _ in range(4)] for _ in range(8)]
        xn = carve(8192, BF16).rearrange("p (c t) -> p c t", c=8)
        xnb = [[Buf() for _ in range(8)] for _ in range(4)]
        small = carve(NSMALL)
        cs = carve(1024).rearrange("p (b d) -> p b d", b=16)
        mats = carve(960, BF16).rearrange("p (m d) -> p m d", m=15)
        ident, ones, tri = mats[:, 0, :], mats[:, 1, :], mats[:, 2, :]
        oh = carve(512, BF16).rearrange("p (j d) -> p j d", j=8)
        negT = carve(2048, BF16).rearrange("p (h t) -> p h t", h=2)
        negTb = Buf()
        der = carve(320)
        g32 = der[:, 0:56]
        neglam = der[:, 56:58]
        gsub = der[:, 58:60]
        lamtmp = der[:, 64:192]
        lamred = der[:, 192:194]
        lamexp = der[:, 194:196]
        km32 = der[:, 200:208]
        gm = der[:, 208:216]
        top8 = der[:, 216:224]
        kmbf = der[:, 224:228].bitcast(BF16)
        nmall = der[:, 228:260].bitcast(BF16).rearrange("p (o j) -> p o j", o=8)
        derb = Buf()
        gmb = Buf()
        womix = [Buf(carve(1024, BF16).rearrange("p (j n) -> p j n", j=2)) for _ in range(2)]
        womix_sem = [newsem() for _ in range(2)]
        wu = [Buf(carve(1536, BF16).rearrange("p (k n) -> p k n", k=8)) for _ in range(1)]
        wu_sem = [newsem() for _ in range(1)]
        poolbd = Buf(carve(128, BF16).rearrange("p (c n) -> p c n", c=2))
        poolbd_sem = newsem()
        sqring = Ring([Buf(carve(256, BF16)) for _ in range(4)])
        rsring = Ring([Buf(carve(512)) for _ in range(2)])
        u_base = off[0]
        constb = Buf()

        off[0] = u_base
        wsets = []
        for i in range(2):
            wa = Buf(carve(SMAX * 512, BF16).rearrange("p (k n) -> p k n", k=8))
            wb = Buf(carve(SMAX * 512, BF16).rearrange("p (k n) -> p k n", k=8))
            wo = Buf(carve(SMAX * 512, BF16).rearrange("p (j n) -> p j n", j=SMAX))
            wsets.append((wa, wb, wo, newsem(), newsem(), newsem()))
        gbufs = []
        for i in range(2):
            gap = carve(SMAX * 256, BF16).rearrange("p (j n) -> p j n", j=SMAX)
            gbufs.append((gap, [Buf() for _ in range(SMAX)]))
        sabufs = Ring([Buf(carve(512)) for _ in range(2)])
        assert off[0] - u_base <= U_WORDS, off[0] - u_base

        off[0] = u_base
        units_b = []
        for i in range(2):
            qkT = carve(2048, BF16).rearrange("p (s t) -> p s t", s=2)
            vtm = carve(1024, BF16).rearrange("p (b d) -> p b d", b=16)
            units_b.append((qkT, vtm, [Buf() for _ in range(16)], [Buf() for _ in range(16)]))
        yring = Ring([(carve(1024, BF16), [Buf() for _ in range(4)]) for _ in range(3)])
        ering = Ring([Buf(carve(256, BF16)) for _ in range(5)])
        tring = Ring([Buf(carve(512)) for _ in range(8)])
        ropet = Ring([(Buf(carve(256)), Buf(carve(256)), Buf(carve(128, BF16))) for _ in range(2)])
        dsqring = Ring([Buf(carve(256, BF16)) for _ in range(2)])
        assert off[0] - u_base <= U_WORDS, off[0] - u_base
        off[0] = u_base + U_WORDS
        assert off[0] <= TOTAL, (off[0], TOTAL)

        pb = [Buf() for _ in range(8)]
        held = [False] * 8
        rr = [0]

        def palloc():
            for i in range(8):
                b = (rr[0] + i) % 8
                if not held[b]:
                    rr[0] = (b + 1) % 8
                    held[b] = True
                    return b
            raise RuntimeError("PSUM exhausted")

        def pfree(b):
            held[b] = False

        def pfix(b):
            assert not held[b], b
            held[b] = True
            return b


        def psbf(b):
            return ps[:, b, :].bitcast(BF16)

        def phase_barrier():
            return [P.q[e][-1] for e in ("pe", "act", "dve", "pool") if P.q[e]]

        c1 = P.dma("sp", lambda e: e.dma_start(out=small, in_=small_d), newsem(), writes=[constb])
        c2 = P.dma("sp", lambda e: e.dma_start(out=cs, in_=cs_d.rearrange("p (b d) -> p b d", b=16)), newsem())
        c3 = P.dma("pool", lambda e: e.dma_start(out=mats, in_=mats_d), newsem())
        c4 = P.dma("pool", lambda e: e.dma_start(out=oh[0:8], in_=oh_d.rearrange("p (j d) -> p j d", j=8)), newsem())
        c5 = P.dma("pool", lambda e: e.dma_start(out=negT[0:8, 0, :], in_=neg_d), newsem())
        c6 = P.dma("pool", lambda e: e.dma_start(out=negT[0:8, 1, :], in_=neg_d), newsem())
        c7 = P.dma("pool", lambda e: e.dma_start(out=nmall, in_=nmi_d.rearrange("p (o j) -> p o j", o=8)), newsem())
        cdeps = [c1, c2, c3, c4, c5, c6, c7]
        cb_ = P.op("dve", lambda e: e.tensor_scalar(out=g32, in0=small[:, SP_NORM:SP_NORM + 56], scalar1=32.0, scalar2=None,
                                                    op0=ALU.mult), writes=[derb], deps=cdeps)
        constb.w = {"dve": cb_}
        for d in cdeps:
            constb.w[id(d)] = d

        def lambda_setup(l):
            lam_init = 0.8 - 0.6 * math.exp(-0.3 * l)
            lv = small[:, SP_LAM + l * 256: SP_LAM + (l + 1) * 256].rearrange("p (a b d) -> p a b d", a=2, b=2)
            P.op("dve", lambda e: e.tensor_tensor(out=lamtmp.rearrange("p (a d) -> p a d", a=2), in0=lv[:, :, 0, :],
                                                  in1=lv[:, :, 1, :], op=ALU.mult), reads=[constb], writes=[derb])
            P.op("dve", lambda e: e.tensor_reduce(out=lamred, in_=lamtmp.rearrange("p (a d) -> p a d", a=2), axis=AX.X,
                                                  op=ALU.add), writes=[derb])
            P.op("act", lambda e: e.activation(out=lamexp, in_=lamred, func=AF.Exp), writes=[derb])
            P.op("dve", lambda e: e.tensor_tensor(out=neglam[:, l:l + 1], in0=lamexp[:, 1:2], in1=lamexp[:, 0:1],
                                                  op=ALU.subtract), writes=[derb])
            P.op("dve", lambda e: e.tensor_scalar(out=neglam[:, l:l + 1], in0=neglam[:, l:l + 1], scalar1=-lam_init,
                                                  scalar2=None, op0=ALU.add), writes=[derb])
            P.op("dve", lambda e: e.tensor_scalar(out=gsub[:, l:l + 1], in0=small[:, SP_SUB + l:SP_SUB + l + 1],
                                                  scalar1=(1.0 - lam_init), scalar2=None, op0=ALU.mult),
                 writes=[derb])

        for l in range(depth):
            lambda_setup(l)

        tsl = lambda t: slice(t * 512, (t + 1) * 512)

        def norm(normidx):
            gcol = normidx * 8
            for t in range(4):
                b = palloc()
                for c in range(8):
                    sq = sqring.next()
                    P.op("pool", lambda e, sq=sq, c=c, t=t: e.tensor_tensor(out=sq.ap, in0=xT[:, c, tsl(t)], in1=xT[:, c, tsl(t)],
                                                                          op=ALU.mult), reads=[xb[c][t]], writes=[sq])
                    P.op("pe", lambda e, sq=sq, c=c, b=b: e.matmul(ps[:, b, :], ones, sq.ap, start=(c == 0), stop=(c == 7)),
                         reads=[sq, constb], writes=[pb[b]])
                rs = rsring.next()
                P.op("act", lambda e, rs=rs, b=b: e.activation(out=rs.ap, in_=ps[:, b, :], func=AF.Sqrt, scale=1.0 / 1024.0,
                                                               bias=EPS), reads=[pb[b]], writes=[rs])
                pfree(b)
                P.op("dve", lambda e, rs=rs: e.reciprocal(out=rs.ap, in_=rs.ap), reads=[rs], writes=[rs])
                yield_rs = rs
                for c in range(8):
                    yield (t, c, rs, gcol)

        def norm_to_xn(normidx):
            for (t, c, rs, gcol) in norm(normidx):
                P.op("dve", lambda e, t=t, c=c, rs=rs, gcol=gcol: e.scalar_tensor_tensor(
                    out=xn[:, c, tsl(t)], in0=xT[:, c, tsl(t)], scalar=small[:, SP_NORM + gcol + c: SP_NORM + gcol + c + 1],
                    in1=rs.ap, op0=ALU.mult, op1=ALU.mult), reads=[xb[c][t], rs, constb], writes=[xnb[t][c]])

        def norm_final():
            for (t, c, rs, gcol) in norm(6):
                P.op("dve", lambda e, t=t, c=c, rs=rs, gcol=gcol: e.scalar_tensor_tensor(
                    out=xT[:, c, tsl(t)], in0=xT[:, c, tsl(t)], scalar=small[:, SP_NORM + gcol + c: SP_NORM + gcol + c + 1],
                    in1=rs.ap, op0=ALU.mult, op1=ALU.mult), reads=[rs, constb], writes=[xb[c][t]])

        wset_rr = [0]
        g_rr = [0]

        def ffn(l, which):
            w_in = w_in_d[which][l]
            w_out = w_out_d[which][l]
            glist = []
            j0 = 0
            for s in groups:
                glist.append((j0, s))
                j0 += s
            loaded = {}

            def load(gi):
                j0, s = glist[gi]
                wa, wb, wo, sa_, sb_, so_ = wsets[wset_rr[0] % 2]
                wset_rr[0] += 1
                P.dma("pool", lambda e: e.dma_start(
                    out=wa.ap[:, :, 0:s * 128], in_=w_in[:, j0 * 128:(j0 + s) * 128].rearrange("(k p) n -> p k n", p=128)),
                    sa_, writes=[wa])
                P.dma("pool", lambda e: e.dma_start(
                    out=wb.ap[:, :, 0:s * 128],
                    in_=w_in[:, D_FF + j0 * 128:D_FF + (j0 + s) * 128].rearrange("(k p) n -> p k n", p=128)),
                    sb_, writes=[wb])
                P.dma("pool", lambda e: e.dma_start(
                    out=wo.ap[:, 0:s, :], in_=w_out[j0 * 128:(j0 + s) * 128, :].rearrange("(j p) n -> p j n", p=128)),
                    so_, writes=[wo])
                loaded[gi] = (wa, wb, wo)

            load(0)
            load(1)
            norm_to_xn(l * 3 + which * 2)
            stages = [(gi, t) for gi in range(len(groups)) for t in range(4)]

            def stage_ab(s, t, wa, wb, wo):
                gap, gb = gbufs[g_rr[0] % 2]
                g_rr[0] += 1
                for jl in range(s):
                    bA = palloc()
                    bB = palloc()
                    for (bank, wsrc) in ((bA, wa), (bB, wb)):
                        for kc in range(8):
                            P.op("pe", lambda e, bank=bank, wsrc=wsrc, kc=kc, jl=jl, t=t: e.matmul(
                                ps[:, bank, :], wsrc.ap[:, kc, jl * 128:(jl + 1) * 128], xn[:, kc, tsl(t)],
                                start=(kc == 0), stop=(kc == 7)), reads=[wsrc] + xnb[t], writes=[pb[bank]])
                    sab = sabufs.next()
                    P.op("act", lambda e, sab=sab, bA=bA: e.activation(out=sab.ap, in_=ps[:, bA, :], func=AF.Silu),
                         reads=[pb[bA]], writes=[sab])
                    pfree(bA)
                    P.op("dve", lambda e, sab=sab, bB=bB, gap=gap, jl=jl: e.tensor_tensor(
                        out=gap[:, jl, :], in0=sab.ap, in1=ps[:, bB, :], op=ALU.mult), reads=[sab, pb[bB]], writes=[gb[jl]])
                    pfree(bB)
                return gap, gb

            def stage_y(s, t, wo, gap, gb):
                for c in range(8):
                    bY = palloc()
                    for jl in range(s):
                        P.op("pe", lambda e, bY=bY, jl=jl, c=c: e.matmul(
                            ps[:, bY, :], wo.ap[:, jl, c * 128:(c + 1) * 128], gap[:, jl, :], start=(jl == 0), stop=(jl == s - 1)),
                            reads=[wo, gb[jl]], writes=[pb[bY]])
                    P.op("dve", lambda e, bY=bY, c=c, t=t: e.scalar_tensor_tensor(
                        out=xT[:, c, tsl(t)], in0=ps[:, bY, :], scalar=0.5, in1=xT[:, c, tsl(t)], op0=ALU.mult, op1=ALU.add),
                        reads=[pb[bY]], writes=[xb[c][t]])
                    pfree(bY)

            prev = None
            for (gi, t) in stages:
                j0, s = glist[gi]
                wa, wb, wo = loaded[gi]
                gap, gb = stage_ab(s, t, wa, wb, wo)
                if prev is not None:
                    stage_y(*prev)
                if t == 0 and gi >= 1 and gi + 1 < len(groups):
                    load(gi + 1)
                prev = (s, t, wo, gap, gb)
            stage_y(*prev)

        unit_rr = [0]
        wu_rr = [0]
        wom_rr = [0]

        def wout_pair(l, ca, yA, yB):
            wo = womix[wom_rr[0] % 2]
            sem = womix_sem[wom_rr[0] % 2]
            wom_rr[0] += 1
            P.dma("pool", lambda e: e.dma_start(
                out=wo.ap, in_=mixo_d[l][ca * 128:(ca + 2) * 128, :].rearrange("(j p) n -> p j n", p=128)), sem, writes=[wo])
            for t in range(4):
                for c in range(8):
                    b = palloc()
                    P.op("pe", lambda e, b=b, c=c, t=t: e.matmul(ps[:, b, :], wo.ap[:, 0, c * 128:(c + 1) * 128],
                                                               yA[0][:, tsl(t)], start=True, stop=False),
                         reads=[wo, yA[1][t]], writes=[pb[b]])
                    P.op("pe", lambda e, b=b, c=c, t=t: e.matmul(ps[:, b, :], wo.ap[:, 1, c * 128:(c + 1) * 128],
                                                               yB[0][:, tsl(t)], start=False, stop=True),
                         reads=[wo, yB[1][t]], writes=[pb[b]])
                    P.op("dve", lambda e, b=b, c=c, t=t: e.tensor_tensor(out=xT[:, c, tsl(t)], in0=ps[:, b, :],
                                                                         in1=xT[:, c, tsl(t)], op=ALU.add),
                         reads=[pb[b]], writes=[xb[c][t]])
                    pfree(b)

        def project_unit(l, colbase, ncols, kind):
            w = wu[0]
            sem = wu_sem[0]
            wu_rr[0] += 1
            P.dma("pool", lambda e: e.dma_start(
                out=w.ap[:, :, 0:ncols], in_=mixw_d[l][:, colbase:colbase + ncols].rearrange("(k p) n -> p k n", p=128)),
                sem, writes=[w])
            qkT, vtm, qkb, vb = units_b[unit_rr[0] % 2]
            unit_rr[0] += 1
            utm = qkT.rearrange("p s t -> p (s t)").rearrange("p (b d) -> p b d", b=16)
            pend = None

            def transposes(tb, qk_tm):
                bt = palloc()
                pv = ps[:, bt, 0:256].rearrange("p (s t) -> p s t", s=2)
                for i in range(2):
                    P.op("pe", lambda e, i=i: e.matmul(pv[:, i, :], qk_tm.ap[:, i * 128:(i + 1) * 128], ident,
                                                       start=True, stop=True),
                         reads=[qk_tm, constb], writes=[pb[bt]])
                P.op("act", lambda e: e.activation(out=qkT[:, :, tb * 128:(tb + 1) * 128], in_=pv, func=AF.Copy),
                     reads=[pb[bt]], writes=[qkb[tb]])
                pfree(bt)

            for tb in range(ntb):
                b = palloc()
                for kc in range(8):
                    P.op("pe", lambda e, b=b, kc=kc, tb=tb: e.matmul(
                        ps[:, b, 0:ncols], xn[:, kc, tb * 128:(tb + 1) * 128], w.ap[:, kc, 0:ncols],
                        start=(kc == 0), stop=(kc == 7)), reads=[w] + xnb[tb // 4], writes=[pb[b]])
                if kind == "P":
                    P.op("act", lambda e, b=b, tb=tb: e.activation(out=utm[:, tb, :], in_=ps[:, b, 0:256], func=AF.Copy),
                         reads=[pb[b]], writes=[qkb[tb]])
                    pfree(b)
                    continue
                t1, t2, qk_tm = ropet.next()
                ps3 = ps[:, b, 0:256].rearrange("p (g d) -> p g d", g=8)
                ps4 = ps[:, b, 0:256].rearrange("p (g h d) -> p g h d", g=4, h=2)
                t24 = t2.ap.rearrange("p (g h d) -> p g h d", g=4, h=2)
                cosb = cs[:, tb:tb + 1, 0:32].to_broadcast([128, 8, 32])
                sinb = cs[:, tb:tb + 1, 32:64].to_broadcast([128, 4, 32])
                P.op("dve", lambda e, t1=t1, ps3=ps3, cosb=cosb: e.tensor_tensor(
                    out=t1.ap.rearrange("p (g d) -> p g d", g=8), in0=ps3, in1=cosb, op=ALU.mult),
                    reads=[pb[b], constb], writes=[t1])
                P.op("dve", lambda e, t24=t24, ps4=ps4, sinb=sinb: e.scalar_tensor_tensor(
                    out=t24[:, :, 0, :], in0=ps4[:, :, 1, :], scalar=-1.0, in1=sinb, op0=ALU.mult, op1=ALU.mult),
                    reads=[pb[b], constb], writes=[t2])
                P.op("dve", lambda e, t24=t24, ps4=ps4, sinb=sinb: e.tensor_tensor(
                    out=t24[:, :, 1, :], in0=ps4[:, :, 0, :], in1=sinb, op=ALU.mult), reads=[pb[b], constb], writes=[t2])
                P.op("dve", lambda e, b=b, tb=tb: e.tensor_copy(out=vtm[:, tb, :], in_=ps[:, b, 256:384]),
                     reads=[pb[b]], writes=[vb[tb]])
                pfree(b)
                P.op("dve", lambda e, t1=t1, t2=t2, qk_tm=qk_tm: e.tensor_tensor(out=qk_tm.ap, in0=t1.ap, in1=t2.ap, op=ALU.add),
                     reads=[t1, t2], writes=[qk_tm])
                if pend is not None:
                    transposes(*pend)
                pend = (tb, qk_tm)
            if pend is not None:
                transposes(*pend)
            return qkT, vtm, qkb, vb, utm

        def pool_unit(l):
            qkT, vtm, qkb, vb, utm = project_unit(l, 0, 256, "P")
            pb_sem = poolbd_sem
            P.dma("pool", lambda e: e.dma_start(out=poolbd.ap, in_=poolbd_d[l].rearrange("c p n -> p c n")), pb_sem, writes=[poolbd])
            ys = []
            for cc in range(2):
                yap, yb = yring.next()
                ys.append((yap, yb))
                for t in range(4):
                    b = palloc()
                    for k in range(4):
                        tb = t * 4 + k
                        for gh in range(2):
                            wi = cc * 2 + gh
                            first = (tb == 0)
                            bm = mats[:, (11 if first else 3) + wi, :]
                            P.op("pe", lambda e, b=b, k=k, gh=gh, tb=tb, bm=bm, first=first, cc=cc: e.matmul(
                                ps[gh * 64:(gh + 1) * 64, b, k * 128:(k + 1) * 128],
                                utm[:, tb, cc * 128 + gh * 64: cc * 128 + (gh + 1) * 64], bm, start=True, stop=first),
                                reads=[qkb[tb], constb], writes=[pb[b]])
                            if not first:
                                P.op("pe", lambda e, b=b, k=k, gh=gh, tb=tb, wi=wi, cc=cc: e.matmul(
                                    ps[gh * 64:(gh + 1) * 64, b, k * 128:(k + 1) * 128],
                                    utm[:, tb - 1, cc * 128 + gh * 64: cc * 128 + (gh + 1) * 64], mats[:, 7 + wi, :],
                                    start=False, stop=True), reads=[qkb[tb - 1], constb], writes=[pb[b]])
                    pl = ering.next()
                    P.op("act", lambda e, pl=pl, b=b: e.activation(out=pl.ap, in_=ps[:, b, :], func=AF.Copy),
                         reads=[pb[b]], writes=[pl])
                    pfree(b)
                    b2 = palloc()
                    P.op("pe", lambda e, pl=pl, b2=b2, cc=cc: e.matmul(ps[:, b2, :], poolbd.ap[:, cc, :], pl.ap, start=True, stop=True),
                         reads=[pl, poolbd], writes=[pb[b2]])
                    P.op("dve", lambda e, b2=b2, t=t, yap=yap, cc=cc: e.tensor_scalar(
                        out=yap[:, tsl(t)], in0=ps[:, b2, :], scalar1=small[:, SP_PSC + l * 2 + cc: SP_PSC + l * 2 + cc + 1],
                        scalar2=None, op0=ALU.mult), reads=[pb[b2], constb], writes=[yb[t]])
                    pfree(b2)
            return ys

        deferred = []

        def run_deferred():
            while deferred:
                deferred.pop(0)()

        def attn_stream(jobs, qk_fn, pv_fn, iter_end, la=4, G=2):
            n = len(jobs)
            st = [None] * n
            for i in range(min(la, n)):
                st[i] = qk_fn(jobs[i])
            i = 0
            countdown = None
            while i < n:
                grp = list(range(i, min(i + G, n)))
                extra = [st[j][0] for j in grp[1:]]
                for gi, j in enumerate(grp):
                    pv_fn(jobs[j], st[j], extra if gi == 0 else [])
                for j in grp:
                    if j + la < n:
                        st[j + la] = qk_fn(jobs[j + la])
                for j in grp:
                    if j in iter_end:
                        iter_end[j]()
                        countdown = 2
                if countdown is not None:
                    countdown -= 1
                    if countdown < 0:
                        run_deferred()
                        countdown = None
                i += G
            run_deferred()

        def diff_unit(l, h):
            qkT, vtm, qkb, vb, _ = project_unit(l, 256 + h * 384, 384, "D")
            yap, yb = yring.next()
            if dlevel <= 1:
                return (yap, yb)
            bO = [pfix(0), pfix(1)]
            bL = [pfix(2), pfix(3)]
            jobs = [(qi, kb, m) for qi in range(4) for kb in range(4 * (qi + 1)) for m in range(2)]

            def qk_fn(job):
                qi, kb, m = job
                r = kb - 4 * qi
                c0 = max(r, 0) * 128
                bS = palloc()
                qdeps = [qkb[kb]] + [qkb[qi * 4 + i] for i in range(c0 // 128, 4)]
                P.op("pe", lambda e: e.matmul(ps[:, bS, c0:512], qkT[m * 64:(m + 1) * 64, 1, kb * 128:(kb + 1) * 128],
                                              qkT[m * 64:(m + 1) * 64, 0, qi * 512 + c0:(qi + 1) * 512], start=True, stop=True),
                     reads=qdeps, writes=[pb[bS]])
                E = ering.next()
                P.op("act", lambda e: e.activation(out=E.ap[:, c0:512], in_=ps[:, bS, c0:512], func=AF.Exp, scale=0.125),
                     reads=[pb[bS]], writes=[E])
                pfree(bS)
                if r >= 0:
                    P.op("pool", lambda e: e.tensor_tensor(out=E.ap[:, c0:c0 + 128], in0=E.ap[:, c0:c0 + 128], in1=tri,
                                                           op=ALU.mult), reads=[E, constb], writes=[E])
                return (E, c0)

            def pv_fn(job, st, extra=()):
                qi, kb, m = job
                nkb = 4 * (qi + 1)
                E, c0 = st
                P.op("pe", lambda e: e.matmul(ps[:, bO[m], c0:512], vtm[:, kb, :], E.ap[:, c0:512],
                                              start=(kb == 0), stop=(kb == nkb - 1)), reads=[E, vb[kb]] + list(extra),
                     writes=[pb[bO[m]]])
                P.op("pe", lambda e: e.matmul(ps[:, bL[m], c0:512], ones, E.ap[:, c0:512],
                                              start=(kb == 0), stop=(kb == nkb - 1)), reads=[E, constb], writes=[pb[bL[m]]])

            def post(qi):
                A = tring.next()
                B = tring.next()
                LA = tring.next()
                LB = tring.next()
                for (T_, bo) in ((A, bO[0]), (B, bO[1])):
                    P.op("dve", lambda e, T_=T_, bo=bo: e.tensor_copy(out=T_.ap, in_=ps[:, bo, :]), reads=[pb[bo]], writes=[T_])
                for (T_, bl) in ((LA, bL[0]), (LB, bL[1])):
                    P.op("act", lambda e, T_=T_, bl=bl: e.activation(out=T_.ap, in_=ps[:, bl, :], func=AF.Ln), reads=[pb[bl]],
                         writes=[T_])

                def part2():
                    for T_ in (LA, LB):
                        P.op("act", lambda e, T_=T_: e.activation(out=T_.ap, in_=T_.ap, func=AF.Exp, scale=-1.0), reads=[T_],
                             writes=[T_])
                    for (T_, R_) in ((A, LA), (B, LB)):
                        P.op("dve", lambda e, T_=T_, R_=R_: e.tensor_tensor(out=T_.ap, in0=T_.ap, in1=R_.ap, op=ALU.mult),
                             reads=[T_, R_], writes=[T_])
                    P.op("dve", lambda e: e.scalar_tensor_tensor(out=A.ap, in0=B.ap, scalar=neglam[:, l:l + 1], in1=A.ap,
                                                                 op0=ALU.mult, op1=ALU.add), reads=[A, B, derb], writes=[A])
                    dsq = dsqring.next()
                    P.op("pool", lambda e: e.tensor_tensor(out=dsq.ap, in0=A.ap, in1=A.ap, op=ALU.mult), reads=[A], writes=[dsq])
                    bs = palloc()
                    P.op("pe", lambda e: e.matmul(ps[:, bs, :], ones, dsq.ap, start=True, stop=True), reads=[dsq, constb],
                         writes=[pb[bs]])
                    P.op("act", lambda e: e.activation(out=LB.ap, in_=ps[:, bs, :], func=AF.Ln, scale=1.0 / 128.0, bias=EPS),
                         reads=[pb[bs]], writes=[LB])
                    pfree(bs)
                    P.op("act", lambda e: e.activation(out=LB.ap, in_=LB.ap, func=AF.Exp, scale=-0.5), reads=[LB], writes=[LB])
                    P.op("dve", lambda e: e.scalar_tensor_tensor(out=yap[:, tsl(qi)], in0=A.ap, scalar=gsub[:, l:l + 1], in1=LB.ap,
                                                                 op0=ALU.mult, op1=ALU.mult), reads=[A, LB, derb], writes=[yb[qi]])
                deferred.append(part2)

            iter_end = {}
            idx = -1
            for qi in range(4):
                idx += 2 * 4 * (qi + 1)
                iter_end[idx] = (lambda qi=qi: post(qi))
            attn_stream(jobs, qk_fn, pv_fn, iter_end, la=4, G=2)
            for b_ in bO + bL:
                pfree(b_)
            return (yap, yb)

        def moba_unit(l, p_):
            qkT, vtm, qkb, vb, _ = project_unit(l, 256 + 4 * 384 + p_ * 384, 384, "M")
            yap, yb = yring.next()
            P.op("dve", lambda e: e.tensor_reduce(out=km32, in_=qkT[:, 1, :].rearrange("p (j t) -> p j t", j=8), axis=AX.X,
                                                  op=ALU.add), reads=qkb, writes=[gmb])
            P.op("dve", lambda e: e.tensor_scalar(out=kmbf, in0=km32, scalar1=1.0 / 256.0, scalar2=None, op0=ALU.mult),
                 reads=[gmb], writes=[gmb])
            bg = palloc()
            for hh in range(2):
                for qb in range(8, 16):
                    idx = hh * 8 + (qb - 8)
                    P.op("pe", lambda e, hh=hh, qb=qb, idx=idx: e.matmul(
                        ps[:, bg, idx * 8:(idx + 1) * 8], qkT[hh * 64:(hh + 1) * 64, 0, qb * 128:(qb + 1) * 128],
                        kmbf[hh * 64:(hh + 1) * 64, :], start=True, stop=True), reads=[qkb[qb], gmb], writes=[pb[bg]])
            nmb = [[Buf() for _ in range(8)] for _ in range(2)]
            nmt = tring.next()
            nmv = nmt.ap[:, 0:64].bitcast(BF16).rearrange("p (i j) -> p i j", i=16)
            for hh in range(2):
                P.op("dve", lambda e: e.memset(gm, -1e30), writes=[gmb])
                for qb in range(8, 16):
                    idx = hh * 8 + (qb - 8)
                    own = qb // 2
                    P.op("dve", lambda e, idx=idx, own=own: e.tensor_copy(out=gm[:, 0:own], in_=ps[:, bg, idx * 8: idx * 8 + own]),
                         reads=[pb[bg], gmb], writes=[gmb])
                    P.op("dve", lambda e: e.max(out=top8, in_=gm), reads=[gmb], writes=[gmb])
                    P.op("dve", lambda e, idx=idx, own=own: e.tensor_copy(out=nmv[:, idx, :], in_=nmall[:, own, :]),
                         reads=[constb, gmb], writes=[nmt])
                    P.op("dve", lambda e, idx=idx, own=own: e.tensor_scalar(
                        out=nmv[:, idx, 0:own], in0=gm[:, 0:own], scalar1=top8[:, 2:3], scalar2=NEG, op0=ALU.is_lt, op1=ALU.mult),
                        reads=[gmb, nmt], writes=[nmt])
            pfree(bg)
            for hh in range(2):
                for half in range(2):
                    bt = palloc()
                    for i in range(4):
                        idx = hh * 8 + half * 4 + i
                        P.op("pe", lambda e, i=i, idx=idx, bt=bt: e.matmul(ps[0:8, bt, i * 128:(i + 1) * 128], nmv[:, idx, :], ident,
                                                                          start=True, stop=True),
                             reads=[nmt, constb], writes=[pb[bt]])
                    P.op("act", lambda e, hh=hh, half=half, bt=bt: e.activation(
                        out=negT[0:8, hh, 1024 + half * 512: 1024 + (half + 1) * 512], in_=ps[0:8, bt, :], func=AF.Copy),
                        reads=[pb[bt]], writes=[negTb])
                    pfree(bt)
            bO = pfix(0)
            bL = pfix(1)
            jobs = [(qi, kb, hh) for qi in range(4) for hh in range(2) for kb in range(4 * (qi + 1))]

            def qk_fn(job):
                qi, kb, hh = job
                r = kb - 4 * qi
                c0 = max(r, 0) * 128
                need_mask = (qi >= 2) and (r < 2)
                bS = palloc()
                qdeps = [qkb[kb]] + [qkb[qi * 4 + i] for i in range(c0 // 128, 4)]
                P.op("pe", lambda e: e.matmul(ps[:, bS, c0:512], qkT[hh * 64:(hh + 1) * 64, 1, kb * 128:(kb + 1) * 128],
                                              qkT[hh * 64:(hh + 1) * 64, 0, qi * 512 + c0:(qi + 1) * 512],
                                              start=True, stop=(not need_mask)), reads=qdeps, writes=[pb[bS]])
                if need_mask:
                    P.op("pe", lambda e: e.matmul(ps[:, bS, c0:512], oh[0:8, kb // 2, :],
                                                  negT[0:8, hh, qi * 512 + c0:(qi + 1) * 512], start=False, stop=True),
                         reads=[negTb, constb], writes=[pb[bS]])
                E = ering.next()
                P.op("act", lambda e: e.activation(out=E.ap[:, c0:512], in_=ps[:, bS, c0:512], func=AF.Exp, scale=0.125),
                     reads=[pb[bS]], writes=[E])
                pfree(bS)
                if r >= 0:
                    P.op("pool", lambda e: e.tensor_tensor(out=E.ap[:, c0:c0 + 128], in0=E.ap[:, c0:c0 + 128], in1=tri,
                                                           op=ALU.mult), reads=[E, constb], writes=[E])
                return (E, c0)

            def pv_fn(job, st, extra=()):
                qi, kb, hh = job
                nkb = 4 * (qi + 1)
                E, c0 = st
                P.op("pe", lambda e: e.matmul(ps[hh * 64:(hh + 1) * 64, bO, c0:512], vtm[:, kb, hh * 64:(hh + 1) * 64],
                                              E.ap[:, c0:512], start=(kb == 0), stop=(kb == nkb - 1)),
                     reads=[E, vb[kb]] + list(extra), writes=[pb[bO]])
                P.op("pe", lambda e: e.matmul(ps[hh * 64:(hh + 1) * 64, bL, c0:512], ones[:, 0:64], E.ap[:, c0:512],
                                              start=(kb == 0), stop=(kb == nkb - 1)), reads=[E, constb], writes=[pb[bL]])

            def post(qi):
                R_ = tring.next()
                OA = tring.next()
                P.op("dve", lambda e: e.tensor_copy(out=OA.ap, in_=ps[:, bO, :]), reads=[pb[bO]], writes=[OA])
                P.op("act", lambda e: e.activation(out=R_.ap, in_=ps[:, bL, :], func=AF.Ln), reads=[pb[bL]], writes=[R_])

                def part2():
                    P.op("act", lambda e: e.activation(out=R_.ap, in_=R_.ap, func=AF.Exp, scale=-1.0), reads=[R_], writes=[R_])
                    P.op("dve", lambda e: e.tensor_tensor(out=yap[:, tsl(qi)], in0=OA.ap, in1=R_.ap, op=ALU.mult),
                         reads=[OA, R_], writes=[yb[qi]])
                deferred.append(part2)

            iter_end = {}
            idx = -1
            for qi in range(4):
                idx += 2 * 4 * (qi + 1)
                iter_end[idx] = (lambda qi=qi: post(qi))
            attn_stream(jobs, qk_fn, pv_fn, iter_end, la=4, G=2)
            pfree(bO)
            pfree(bL)
            return (yap, yb)

        def mix(l):
            norm_to_xn(l * 3 + 1)
            if "P" in parts:
                ya = pool_unit(l)
                wout_pair(l, 0, ya[0], ya[1])
            if "D" in parts:
                d0 = diff_unit(l, 0)
                if nd <= 1:
                    return
                d1 = diff_unit(l, 1)
                if dlevel > 2:
                    wout_pair(l, 2, d0, d1)
                d2 = diff_unit(l, 2)
                d3 = diff_unit(l, 3)
                if dlevel > 2:
                    wout_pair(l, 4, d2, d3)
            if "M" in parts:
                m0 = moba_unit(l, 0)
                m1 = moba_unit(l, 1)
                wout_pair(l, 6, m0, m1)

        xsems = [newsem() for _ in range(8)]
        stores = []
        for seq in range(nseq):
            for c in range(8):
                P.dma("sp", lambda e, c=c, seq=seq: e.dma_start(out=xT[:, c, :], in_=xT_d[seq][c * 128:(c + 1) * 128, :]),
                      xsems[c], writes=xb[c])
            done = False
            for l in range(depth):
                if not skip_ffn:
                    ffn(l, 0)
                if stop == ("ffn1", l):
                    done = True
                    break
                bar = phase_barrier()
                for e_ in ("pe", "act", "dve", "pool"):
                    P.op(e_, (lambda e: e.nop()) if e_ != "pe" else (lambda e: e.nop()), deps=bar)
                mix(l)
                if stop == ("mix", l):
                    done = True
                    break
                bar = phase_barrier()
                for e_ in ("pe", "act", "dve", "pool"):
                    P.op(e_, lambda e: e.nop(), deps=bar)
                ffn(l, 1)
                if stop == ("ffn2", l):
                    done = True
                    break
            if not done:
                norm_final()
            for c in range(8):
                stores.append(P.dma("sp", lambda e, c=c, seq=seq: e.dma_start(out=out_d[seq][c * 128:(c + 1) * 128, :],
                                                                             in_=xT[:, c, :]), xsems[c], reads=xb[c]))
        P.op("sp", lambda e: e.nop(), deps=stores)
        P.emit(block, esems)
    return nc


_CACHE = {}


def _prep_shared(inp, depth):
    cs, mats, oh, negc, nmi = _const_tables()
    perm = _mix_perm()
    shared = {
        "ffn1_w_in": np.ascontiguousarray(inp["ffn1_w_in"][:depth], np.float32),
        "ffn2_w_in": np.ascontiguousarray(inp["ffn2_w_in"][:depth], np.float32),
        "ffn1_w_out": np.ascontiguousarray(inp["ffn1_w_out"][:depth], np.float32),
        "ffn2_w_out": np.ascontiguousarray(inp["ffn2_w_out"][:depth], np.float32),
        "mixw": np.ascontiguousarray(inp["mix_w_in"][:depth][:, :, perm], np.float32),
        "mixo": np.ascontiguousarray(inp["mix_w_out"][:depth], np.float32),
        "poolbd": _pool_bd(np.asarray(inp["pool_w"], np.float32), depth),
        "smallp": _small_params(inp, depth),
        "c_cs": cs, "c_mats": mats, "c_oh": oh, "c_neg": negc, "c_nmi": nmi,
    }
    return shared


def kernel(**inputs):
    inp = {k: np.asarray(v) for k, v in inputs.items()}
    x = inp["x"].astype(np.float32, copy=False)
    ncores = 8
    nseq = x.shape[0] // ncores
    if "nc" not in _CACHE:
        _CACHE["nc"] = build_program(nseq=nseq, depth=DEPTH)
    nc = _CACHE["nc"]
    shared = _prep_shared(inp, DEPTH)
    in_maps = []
    for i in range(ncores):
        xs = x[i * nseq:(i + 1) * nseq]
        m = dict(shared)
        m["xT"] = np.ascontiguousarray(xs.transpose(0, 2, 1))
        in_maps.append(m)
    res = run_bass_kernel_spmd(nc, in_maps, core_ids=list(range(ncores)))
    outs = [np.asarray(r["outT"]).transpose(0, 2, 1) for r in res.results]
    return np.ascontiguousarray(np.concatenate(outs, axis=0), dtype=np.float32)
```

```python
import math
import numpy as np
import concourse.bass as bass
import concourse.mybir as mybir
from concourse.bass_utils import run_bass_kernel_spmd

F32 = mybir.dt.float32
BF16 = mybir.dt.bfloat16
ALU = mybir.AluOpType
AF = mybir.ActivationFunctionType
AX = mybir.AxisListType

D_MODEL = 1024
SEQ = 2048
DEPTH = 2
D_FF = 2816
NFF = D_FF // 128
IN_COLS = 2560
POOL_WINDOWS = (2, 4, 8, 16)
EPS = 1e-6
NEG = -30000.0
ENGS = ("pe", "act", "dve", "pool", "sp")
EPOCH = 8192


class Op:
    __slots__ = ("eng", "fn", "deps", "signal", "sem", "count", "is_dma", "idx")

    def __init__(self, eng, fn, deps, is_dma=False, sem=None):
        self.eng = eng
        self.fn = fn
        self.deps = deps
        self.signal = False
        self.sem = sem
        self.count = None
        self.is_dma = is_dma


class Buf:
    __slots__ = ("ap", "w", "r")

    def __init__(self, ap=None):
        self.ap = ap
        self.w = {}
        self.r = {}


class Prog:
    def __init__(self, nc):
        self.nc = nc
        self.q = {e: [] for e in ENGS}
        self.dma_sem_counts = {}

    def _mk(self, eng, fn, reads, writes, deps, is_dma, sem):
        d = [x for x in deps if x is not None]
        for b in reads:
            d.extend(b.w.values())
        for b in writes:
            d.extend(b.w.values())
            d.extend(b.r.values())
        if eng == "pe" and not is_dma:
            d = [x for x in d if x.is_dma or x.eng != "pe"]
        seen = set()
        dd = []
        for x in d:
            if id(x) not in seen:
                seen.add(id(x))
                dd.append(x)
        o = Op(eng, fn, dd, is_dma=is_dma, sem=sem)
        for x in dd:
            x.signal = True
        key = id(o) if is_dma else eng
        for b in reads:
            b.r[key] = o
        for b in writes:
            b.w = {key: o}
            b.r = {}
        self.q[eng].append(o)
        return o

    def op(self, eng, fn, reads=(), writes=(), deps=()):
        return self._mk(eng, fn, reads, writes, deps, False, None)

    def dma(self, eng, fn, sem, reads=(), writes=(), deps=()):
        o = self._mk(eng, fn, reads, writes, deps, True, sem)
        o.signal = True
        c = self.dma_sem_counts.get(id(sem), 0) + 16
        self.dma_sem_counts[id(sem)] = c
        o.count = c
        return o

    def emit(self, block, esems):
        for e in ENGS:
            c = 0
            for o in self.q[e]:
                if o.is_dma:
                    continue
                if o.signal:
                    o.sem = esems[e][c // EPOCH]
                    o.count = c % EPOCH + 1
                    c += 1

        def run(e, engobj):
            waited = {}
            for o in self.q[e]:
                for d in o.deps:
                    k = id(d.sem)
                    if waited.get(k, 0) >= d.count:
                        continue
                    engobj.wait_ge(d.sem, d.count)
                    waited[k] = d.count
                ins = o.fn(engobj)
                if o.signal:
                    ins.then_inc(o.sem, 16 if o.is_dma else 1)

        @block.tensor
        def _(eng):
            run("pe", eng)

        @block.scalar
        def _(eng):
            run("act", eng)

        @block.vector
        def _(eng):
            run("dve", eng)

        @block.gpsimd
        def _(eng):
            run("pool", eng)

        @block.sync
        def _(eng):
            run("sp", eng)


class Ring:
    def __init__(self, items):
        self.items = items
        self.i = 0

    def next(self):
        it = self.items[self.i % len(self.items)]
        self.i += 1
        return it


def _const_tables():
    inv = 10000.0 ** (-np.arange(0, 64, 2, dtype=np.float32) / 64.0)
    ang = np.arange(SEQ, dtype=np.float32)[:, None] * inv[None, :]
    cos = np.cos(ang).astype(np.float32)
    sin = np.sin(ang).astype(np.float32)
    cs = np.zeros((128, 16, 64), np.float32)
    cs[:, :, 0:32] = cos.reshape(16, 128, 32).transpose(1, 0, 2)
    cs[:, :, 32:64] = sin.reshape(16, 128, 32).transpose(1, 0, 2)
    mats = np.zeros((128, 15, 128), np.float32)
    mats[:, 0, :] = np.eye(128, dtype=np.float32)
    mats[:, 1, :] = 1.0
    p = np.arange(128)[:, None]
    f = np.arange(128)[None, :]
    mats[:, 2, :] = (p <= f).astype(np.float32)
    for wi, w in enumerate(POOL_WINDOWS):
        tp, t = p, f
        inwin = ((tp <= t) & (tp > t - w)).astype(np.float32)
        eye = (tp == t).astype(np.float32)
        mats[:, 3 + wi, :] = inwin / w - eye
        mats[:, 7 + wi, :] = ((tp - 128) > (t - w)).astype(np.float32) / w
        cnt = np.minimum(t + 1, w).astype(np.float32)
        mats[:, 11 + wi, :] = inwin / cnt - eye
    oh = np.zeros((8, 8, 128), np.float32)
    for j in range(8):
        oh[j, j, :] = 1.0
    q = np.arange(SEQ)
    own = q // 256
    negc = np.where(np.arange(8)[:, None] <= own[None, :], 0.0, NEG).astype(np.float32)
    nminit = np.zeros((128, 8, 8), np.float32)
    for o in range(8):
        nminit[:, o, o + 1:] = NEG
    return cs.reshape(128, 1024), mats, oh.reshape(8, 1024), negc, nminit.reshape(128, 64)


def _mix_perm():
    perm = list(range(0, 256))
    for h in range(4):
        perm += list(range(256 + h * 128, 256 + (h + 1) * 128))
        perm += list(range(768 + h * 128, 768 + (h + 1) * 128))
        perm += list(range(1280 + h * 128, 1280 + (h + 1) * 128))
    for p_ in range(2):
        perm += list(range(1792 + p_ * 128, 1792 + (p_ + 1) * 128))
        perm += list(range(2048 + p_ * 128, 2048 + (p_ + 1) * 128))
        perm += list(range(2304 + p_ * 128, 2304 + (p_ + 1) * 128))
    return np.array(perm)


NSMALL = 576
SP_NORM = 0
SP_PSC = 56
SP_SUB = 60
SP_LAM = 64


def _small_params(inp, depth):
    sp = np.zeros((128, NSMALL), np.float32)
    names = ["ffn1_norm", "mix_norm", "ffn2_norm"]
    for l in range(depth):
        for k, nm in enumerate(names):
            sp[:, SP_NORM + (l * 3 + k) * 8: SP_NORM + (l * 3 + k + 1) * 8] = inp[nm][l].reshape(8, 128).T
        sp[:, SP_PSC + l * 2: SP_PSC + l * 2 + 2] = inp["pool_scale"][l].reshape(2, 128).T
        sp[:, SP_SUB + l] = inp["diff_subln"][l]
        sp[:, SP_LAM + l * 256: SP_LAM + (l + 1) * 256] = inp["diff_lambda"][l].reshape(1, 256)
    sp[:, SP_NORM + 48: SP_NORM + 56] = inp["final_norm"].reshape(8, 128).T
    return sp


def _pool_bd(pool_w, depth):
    bd = np.zeros((depth, 2, 128, 128), np.float32)
    for l in range(depth):
        for g in range(4):
            cc, gh = g // 2, g % 2
            bd[l, cc, gh * 64:(gh + 1) * 64, gh * 64:(gh + 1) * 64] = pool_w[l, g]
    return bd


def build_program(nseq=2, depth=DEPTH, groups=(4, 4, 4, 4, 3, 3), stop=None, parts=("P", "D", "M"), dlevel=9, skip_ffn=False, ntb=16, nd=4):
    assert sum(groups) == NFF
    SMAX = max(groups)
    nc = bass.Bass("TRN2", target_bir_lowering=False)
    dr = lambda name, shape: nc.dram_tensor(name, list(shape), F32, kind="ExternalInput").ap()
    xT_d = dr("xT", (nseq, D_MODEL, SEQ))
    w_in_d = [dr("ffn1_w_in", (depth, D_MODEL, 2 * D_FF)), dr("ffn2_w_in", (depth, D_MODEL, 2 * D_FF))]
    w_out_d = [dr("ffn1_w_out", (depth, D_FF, D_MODEL)), dr("ffn2_w_out", (depth, D_FF, D_MODEL))]
    mixw_d = dr("mixw", (depth, D_MODEL, IN_COLS))
    mixo_d = dr("mixo", (depth, D_MODEL, D_MODEL))
    poolbd_d = dr("poolbd", (depth, 2, 128, 128))
    small_d = dr("smallp", (128, NSMALL))
    cs_d = dr("c_cs", (128, 1024))
    mats_d = dr("c_mats", (128, 15, 128))
    oh_d = dr("c_oh", (8, 1024))
    neg_d = dr("c_neg", (8, SEQ))
    nmi_d = dr("c_nmi", (128, 64))
    out_d = nc.dram_tensor("outT", [nseq, D_MODEL, SEQ], F32, kind="ExternalOutput").ap()

    U_WORDS = 16384
    TOTAL = 16384 + 8192 + NSMALL + 1024 + 960 + 512 + 2048 + 320 + 2048 + 1536 + 128 + 2048 + U_WORDS
    nsem_e = {"pe": 6, "act": 4, "dve": 6, "pool": 4, "sp": 1}
    NDMA = 30

    from contextlib import ExitStack
    with ExitStack() as es:
        arena = es.enter_context(nc.sbuf_tensor("arena", [128, TOTAL], F32))
        ps = es.enter_context(nc.psum_tensor("ps", [128, 8, 512], F32))
        esems = {e: [es.enter_context(nc.semaphore(f"s_{e}{i}")) for i in range(n)] for e, n in nsem_e.items()}
        dsems = [es.enter_context(nc.semaphore(f"d{i}")) for i in range(NDMA)]
        block = es.enter_context(nc.Block())
        dsem_i = [0]

        def newsem():
            s = dsems[dsem_i[0]]
            dsem_i[0] += 1
            return s

        P = Prog(nc)
        off = [0]

        def carve(nwords, dt=F32):
            v = arena[:, off[0]:off[0] + nwords]
            off[0] += nwords
            if dt is not F32:
                v = v.bitcast(dt)
            return v

        xT = carve(16384).rearrange("p (c t) -> p c t", c=8)
        xb = [[Buf() for _ in range(4)] for _ in range(8)]
        xn = carve(8192, BF16).rearrange("p (c t) -> p c t", c=8)
        xnb = [[Buf() for _ in range(8)] for _ in range(4)]
        small = carve(NSMALL)
        cs = carve(1024).rearrange("p (b d) -> p b d", b=16)
        mats = carve(960, BF16).rearrange("p (m d) -> p m d", m=15)
        ident, ones, tri = mats[:, 0, :], mats[:, 1, :], mats[:, 2, :]
        oh = carve(512, BF16).rearrange("p (j d) -> p j d", j=8)
        negT = carve(2048, BF16).rearrange("p (h t) -> p h t", h=2)
        negTb = Buf()
        der = carve(320)
        g32 = der[:, 0:56]
        neglam = der[:, 56:58]
        gsub = der[:, 58:60]
        lamtmp = der[:, 64:192]
        lamred = der[:, 192:194]
        lamexp = der[:, 194:196]
        km32 = der[:, 200:208]
        gm = der[:, 208:216]
        top8 = der[:, 216:224]
        kmbf = der[:, 224:228].bitcast(BF16)
        nmall = der[:, 228:260].bitcast(BF16).rearrange("p (o j) -> p o j", o=8)
        derb = Buf()
        gmb = Buf()
        womix = [Buf(carve(1024, BF16).rearrange("p (j n) -> p j n", j=2)) for _ in range(2)]
        womix_sem = [newsem() for _ in range(2)]
        wu = [Buf(carve(1536, BF16).rearrange("p (k n) -> p k n", k=8)) for _ in range(1)]
        wu_sem = [newsem() for _ in range(1)]
        poolbd = Buf(carve(128, BF16).rearrange("p (c n) -> p c n", c=2))
        poolbd_sem = newsem()
        sqring = Ring([Buf(carve(256, BF16)) for _ in range(4)])
        rsring = Ring([Buf(carve(512)) for _ in range(2)])
        u_base = off[0]
        constb = Buf()

        off[0] = u_base
        wsets = []
        for i in range(2):
            wa = Buf(carve(SMAX * 512, BF16).rearrange("p (k n) -> p k n", k=8))
            wb = Buf(carve(SMAX * 512, BF16).rearrange("p (k n) -> p k n", k=8))
            wo = Buf(carve(SMAX * 512, BF16).rearrange("p (j n) -> p j n", j=SMAX))
            wsets.append((wa, wb, wo, newsem(), newsem(), newsem()))
        gbufs = []
        for i in range(2):
            gap = carve(SMAX * 256, BF16).rearrange("p (j n) -> p j n", j=SMAX)
            gbufs.append((gap, [Buf() for _ in range(SMAX)]))
        sabufs = Ring([Buf(carve(512)) for _ in range(2)])
        assert off[0] - u_base <= U_WORDS, off[0] - u_base

        off[0] = u_base
        units_b = []
        for i in range(2):
            qkT = carve(2048, BF16).rearrange("p (s t) -> p s t", s=2)
            vtm = carve(1024, BF16).rearrange("p (b d) -> p b d", b=16)
            units_b.append((qkT, vtm, [Buf() for _ in range(16)], [Buf() for _ in range(16)]))
        yring = Ring([(carve(1024, BF16), [Buf() for _ in range(4)]) for _ in range(3)])
        ering = Ring([Buf(carve(256, BF16)) for _ in range(5)])
        tring = Ring([Buf(carve(512)) for _ in range(8)])
        ropet = Ring([(Buf(carve(256)), Buf(carve(256)), Buf(carve(128, BF16))) for _ in range(2)])
        dsqring = Ring([Buf(carve(256, BF16)) for _ in range(2)])
        assert off[0] - u_base <= U_WORDS, off[0] - u_base
        off[0] = u_base + U_WORDS
        assert off[0] <= TOTAL, (off[0], TOTAL)

        pb = [Buf() for _ in range(8)]
        held = [False] * 8
        rr = [0]

        def palloc():
            for i in range(8):
                b = (rr[0] + i) % 8
                if not held[b]:
                    rr[0] = (b + 1) % 8
                    held[b] = True
                    return b
            raise RuntimeError("PSUM exhausted")

        def pfree(b):
            held[b] = False

        def pfix(b):
            assert not held[b], b
            held[b] = True
            return b


        def psbf(b):
            return ps[:, b, :].bitcast(BF16)

        def phase_barrier():
            return [P.q[e][-1] for e in ("pe", "act", "dve", "pool") if P.q[e]]

        c1 = P.dma("sp", lambda e: e.dma_start(out=small, in_=small_d), newsem(), writes=[constb])
        c2 = P.dma("sp", lambda e: e.dma_start(out=cs, in_=cs_d.rearrange("p (b d) -> p b d", b=16)), newsem())
        c3 = P.dma("pool", lambda e: e.dma_start(out=mats, in_=mats_d), newsem())
        c4 = P.dma("pool", lambda e: e.dma_start(out=oh[0:8], in_=oh_d.rearrange("p (j d) -> p j d", j=8)), newsem())
        c5 = P.dma("pool", lambda e: e.dma_start(out=negT[0:8, 0, :], in_=neg_d), newsem())
        c6 = P.dma("pool", lambda e: e.dma_start(out=negT[0:8, 1, :], in_=neg_d), newsem())
        c7 = P.dma("pool", lambda e: e.dma_start(out=nmall, in_=nmi_d.rearrange("p (o j) -> p o j", o=8)), newsem())
        cdeps = [c1, c2, c3, c4, c5, c6, c7]
        cb_ = P.op("dve", lambda e: e.tensor_scalar(out=g32, in0=small[:, SP_NORM:SP_NORM + 56], scalar1=32.0, scalar2=None,
                                                    op0=ALU.mult), writes=[derb], deps=cdeps)
        constb.w = {"dve": cb_}
        for d in cdeps:
            constb.w[id(d)] = d

        def lambda_setup(l):
            lam_init = 0.8 - 0.6 * math.exp(-0.3 * l)
            lv = small[:, SP_LAM + l * 256: SP_LAM + (l + 1) * 256].rearrange("p (a b d) -> p a b d", a=2, b=2)
            P.op("dve", lambda e: e.tensor_tensor(out=lamtmp.rearrange("p (a d) -> p a d", a=2), in0=lv[:, :, 0, :],
                                                  in1=lv[:, :, 1, :], op=ALU.mult), reads=[constb], writes=[derb])
            P.op("dve", lambda e: e.tensor_reduce(out=lamred, in_=lamtmp.rearrange("p (a d) -> p a d", a=2), axis=AX.X,
                                                  op=ALU.add), writes=[derb])
            P.op("act", lambda e: e.activation(out=lamexp, in_=lamred, func=AF.Exp), writes=[derb])
            P.op("dve", lambda e: e.tensor_tensor(out=neglam[:, l:l + 1], in0=lamexp[:, 1:2], in1=lamexp[:, 0:1],
                                                  op=ALU.subtract), writes=[derb])
            P.op("dve", lambda e: e.tensor_scalar(out=neglam[:, l:l + 1], in0=neglam[:, l:l + 1], scalar1=-lam_init,
                                                  scalar2=None, op0=ALU.add), writes=[derb])
            P.op("dve", lambda e: e.tensor_scalar(out=gsub[:, l:l + 1], in0=small[:, SP_SUB + l:SP_SUB + l + 1],
                                                  scalar1=(1.0 - lam_init), scalar2=None, op0=ALU.mult),
                 writes=[derb])

        for l in range(depth):
            lambda_setup(l)

        tsl = lambda t: slice(t * 512, (t + 1) * 512)

        def norm(normidx):
            gcol = normidx * 8
            for t in range(4):
                b = palloc()
                for c in range(8):
                    sq = sqring.next()
                    P.op("pool", lambda e, sq=sq, c=c, t=t: e.tensor_tensor(out=sq.ap, in0=xT[:, c, tsl(t)], in1=xT[:, c, tsl(t)],
                                                                          op=ALU.mult), reads=[xb[c][t]], writes=[sq])
                    P.op("pe", lambda e, sq=sq, c=c, b=b: e.matmul(ps[:, b, :], ones, sq.ap, start=(c == 0), stop=(c == 7)),
                         reads=[sq, constb], writes=[pb[b]])
                rs = rsring.next()
                P.op("act", lambda e, rs=rs, b=b: e.activation(out=rs.ap, in_=ps[:, b, :], func=AF.Sqrt, scale=1.0 / 1024.0,
                                                               bias=EPS), reads=[pb[b]], writes=[rs])
                pfree(b)
                P.op("dve", lambda e, rs=rs: e.reciprocal(out=rs.ap, in_=rs.ap), reads=[rs], writes=[rs])
                yield_rs = rs
                for c in range(8):
                    yield (t, c, rs, gcol)

        def norm_to_xn(normidx):
            for (t, c, rs, gcol) in norm(normidx):
                P.op("dve", lambda e, t=t, c=c, rs=rs, gcol=gcol: e.scalar_tensor_tensor(
                    out=xn[:, c, tsl(t)], in0=xT[:, c, tsl(t)], scalar=small[:, SP_NORM + gcol + c: SP_NORM + gcol + c + 1],
                    in1=rs.ap, op0=ALU.mult, op1=ALU.mult), reads=[xb[c][t], rs, constb], writes=[xnb[t][c]])

        def norm_final():
            for (t, c, rs, gcol) in norm(6):
                P.op("dve", lambda e, t=t, c=c, rs=rs, gcol=gcol: e.scalar_tensor_tensor(
                    out=xT[:, c, tsl(t)], in0=xT[:, c, tsl(t)], scalar=small[:, SP_NORM + gcol + c: SP_NORM + gcol + c + 1],
                    in1=rs.ap, op0=ALU.mult, op1=ALU.mult), reads=[rs, constb], writes=[xb[c][t]])

        wset_rr = [0]
        g_rr = [0]

        def ffn(l, which):
            w_in = w_in_d[which][l]
            w_out = w_out_d[which][l]
            glist = []
            j0 = 0
            for s in groups:
                glist.append((j0, s))
                j0 += s
            loaded = {}

            def load(gi):
                j0, s = glist[gi]
                wa, wb, wo, sa_, sb_, so_ = wsets[wset_rr[0] % 2]
                wset_rr[0] += 1
                P.dma("pool", lambda e: e.dma_start(
                    out=wa.ap[:, :, 0:s * 128], in_=w_in[:, j0 * 128:(j0 + s) * 128].rearrange("(k p) n -> p k n", p=128)),
                    sa_, writes=[wa])
                P.dma("pool", lambda e: e.dma_start(
                    out=wb.ap[:, :, 0:s * 128],
                    in_=w_in[:, D_FF + j0 * 128:D_FF + (j0 + s) * 128].rearrange("(k p) n -> p k n", p=128)),
                    sb_, writes=[wb])
                P.dma("pool", lambda e: e.dma_start(
                    out=wo.ap[:, 0:s, :], in_=w_out[j0 * 128:(j0 + s) * 128, :].rearrange("(j p) n -> p j n", p=128)),
                    so_, writes=[wo])
                loaded[gi] = (wa, wb, wo)

            load(0)
            load(1)
            norm_to_xn(l * 3 + which * 2)
            stages = [(gi, t) for gi in range(len(groups)) for t in range(4)]

            def stage_ab(s, t, wa, wb, wo):
                gap, gb = gbufs[g_rr[0] % 2]
                g_rr[0] += 1
                for jl in range(s):
                    bA = palloc()
                    bB = palloc()
                    for (bank, wsrc) in ((bA, wa), (bB, wb)):
                        for kc in range(8):
                            P.op("pe", lambda e, bank=bank, wsrc=wsrc, kc=kc, jl=jl, t=t: e.matmul(
                                ps[:, bank, :], wsrc.ap[:, kc, jl * 128:(jl + 1) * 128], xn[:, kc, tsl(t)],
                                start=(kc == 0), stop=(kc == 7)), reads=[wsrc] + xnb[t], writes=[pb[bank]])
                    sab = sabufs.next()
                    P.op("act", lambda e, sab=sab, bA=bA: e.activation(out=sab.ap, in_=ps[:, bA, :], func=AF.Silu),
                         reads=[pb[bA]], writes=[sab])
                    pfree(bA)
                    P.op("dve", lambda e, sab=sab, bB=bB, gap=gap, jl=jl: e.tensor_tensor(
                        out=gap[:, jl, :], in0=sab.ap, in1=ps[:, bB, :], op=ALU.mult), reads=[sab, pb[bB]], writes=[gb[jl]])
                    pfree(bB)
                return gap, gb

            def stage_y(s, t, wo, gap, gb):
                for c in range(8):
                    bY = palloc()
                    for jl in range(s):
                        P.op("pe", lambda e, bY=bY, jl=jl, c=c: e.matmul(
                            ps[:, bY, :], wo.ap[:, jl, c * 128:(c + 1) * 128], gap[:, jl, :], start=(jl == 0), stop=(jl == s - 1)),
                            reads=[wo, gb[jl]], writes=[pb[bY]])
                    P.op("dve", lambda e, bY=bY, c=c, t=t: e.scalar_tensor_tensor(
                        out=xT[:, c, tsl(t)], in0=ps[:, bY, :], scalar=0.5, in1=xT[:, c, tsl(t)], op0=ALU.mult, op1=ALU.add),
                        reads=[pb[bY]], writes=[xb[c][t]])
                    pfree(bY)

            prev = None
            for (gi, t) in stages:
                j0, s = glist[gi]
                wa, wb, wo = loaded[gi]
                gap, gb = stage_ab(s, t, wa, wb, wo)
                if prev is not None:
                    stage_y(*prev)
                if t == 0 and gi >= 1 and gi + 1 < len(groups):
                    load(gi + 1)
                prev = (s, t, wo, gap, gb)
            stage_y(*prev)

        unit_rr = [0]
        wu_rr = [0]
        wom_rr = [0]

        def wout_pair(l, ca, yA, yB):
            wo = womix[wom_rr[0] % 2]
            sem = womix_sem[wom_rr[0] % 2]
            wom_rr[0] += 1
            P.dma("pool", lambda e: e.dma_start(
                out=wo.ap, in_=mixo_d[l][ca * 128:(ca + 2) * 128, :].rearrange("(j p) n -> p j n", p=128)), sem, writes=[wo])
            for t in range(4):
                for c in range(8):
                    b = palloc()
                    P.op("pe", lambda e, b=b, c=c, t=t: e.matmul(ps[:, b, :], wo.ap[:, 0, c * 128:(c + 1) * 128],
                                                               yA[0][:, tsl(t)], start=True, stop=False),
                         reads=[wo, yA[1][t]], writes=[pb[b]])
                    P.op("pe", lambda e, b=b, c=c, t=t: e.matmul(ps[:, b, :], wo.ap[:, 1, c * 128:(c + 1) * 128],
                                                               yB[0][:, tsl(t)], start=False, stop=True),
                         reads=[wo, yB[1][t]], writes=[pb[b]])
                    P.op("dve", lambda e, b=b, c=c, t=t: e.tensor_tensor(out=xT[:, c, tsl(t)], in0=ps[:, b, :],
                                                                         in1=xT[:, c, tsl(t)], op=ALU.add),
                         reads=[pb[b]], writes=[xb[c][t]])
                    pfree(b)

        def project_unit(l, colbase, ncols, kind):
            w = wu[0]
            sem = wu_sem[0]
            wu_rr[0] += 1
            P.dma("pool", lambda e: e.dma_start(
                out=w.ap[:, :, 0:ncols], in_=mixw_d[l][:, colbase:colbase + ncols].rearrange("(k p) n -> p k n", p=128)),
                sem, writes=[w])
            qkT, vtm, qkb, vb = units_b[unit_rr[0] % 2]
            unit_rr[0] += 1
            utm = qkT.rearrange("p s t -> p (s t)").rearrange("p (b d) -> p b d", b=16)
            pend = None

            def transposes(tb, qk_tm):
                bt = palloc()
                pv = ps[:, bt, 0:256].rearrange("p (s t) -> p s t", s=2)
                for i in range(2):
                    P.op("pe", lambda e, i=i: e.matmul(pv[:, i, :], qk_tm.ap[:, i * 128:(i + 1) * 128], ident,
                                                       start=True, stop=True),
                         reads=[qk_tm, constb], writes=[pb[bt]])
                P.op("act", lambda e: e.activation(out=qkT[:, :, tb * 128:(tb + 1) * 128], in_=pv, func=AF.Copy),
                     reads=[pb[bt]], writes=[qkb[tb]])
                pfree(bt)

            for tb in range(ntb):
                b = palloc()
                for kc in range(8):
                    P.op("pe", lambda e, b=b, kc=kc, tb=tb: e.matmul(
                        ps[:, b, 0:ncols], xn[:, kc, tb * 128:(tb + 1) * 128], w.ap[:, kc, 0:ncols],
                        start=(kc == 0), stop=(kc == 7)), reads=[w] + xnb[tb // 4], writes=[pb[b]])
                if kind == "P":
                    P.op("act", lambda e, b=b, tb=tb: e.activation(out=utm[:, tb, :], in_=ps[:, b, 0:256], func=AF.Copy),
                         reads=[pb[b]], writes=[qkb[tb]])
                    pfree(b)
                    continue
                t1, t2, qk_tm = ropet.next()
                ps3 = ps[:, b, 0:256].rearrange("p (g d) -> p g d", g=8)
                ps4 = ps[:, b, 0:256].rearrange("p (g h d) -> p g h d", g=4, h=2)
                t24 = t2.ap.rearrange("p (g h d) -> p g h d", g=4, h=2)
                cosb = cs[:, tb:tb + 1, 0:32].to_broadcast([128, 8, 32])
                sinb = cs[:, tb:tb + 1, 32:64].to_broadcast([128, 4, 32])
                P.op("dve", lambda e, t1=t1, ps3=ps3, cosb=cosb: e.tensor_tensor(
                    out=t1.ap.rearrange("p (g d) -> p g d", g=8), in0=ps3, in1=cosb, op=ALU.mult),
                    reads=[pb[b], constb], writes=[t1])
                P.op("dve", lambda e, t24=t24, ps4=ps4, sinb=sinb: e.scalar_tensor_tensor(
                    out=t24[:, :, 0, :], in0=ps4[:, :, 1, :], scalar=-1.0, in1=sinb, op0=ALU.mult, op1=ALU.mult),
                    reads=[pb[b], constb], writes=[t2])
                P.op("dve", lambda e, t24=t24, ps4=ps4, sinb=sinb: e.tensor_tensor(
                    out=t24[:, :, 1, :], in0=ps4[:, :, 0, :], in1=sinb, op=ALU.mult), reads=[pb[b], constb], writes=[t2])
                P.op("dve", lambda e, b=b, tb=tb: e.tensor_copy(out=vtm[:, tb, :], in_=ps[:, b, 256:384]),
                     reads=[pb[b]], writes=[vb[tb]])
                pfree(b)
                P.op("dve", lambda e, t1=t1, t2=t2, qk_tm=qk_tm: e.tensor_tensor(out=qk_tm.ap, in0=t1.ap, in1=t2.ap, op=ALU.add),
                     reads=[t1, t2], writes=[qk_tm])
                if pend is not None:
                    transposes(*pend)
                pend = (tb, qk_tm)
            if pend is not None:
                transposes(*pend)
            return qkT, vtm, qkb, vb, utm

        def pool_unit(l):
            qkT, vtm, qkb, vb, utm = project_unit(l, 0, 256, "P")
            pb_sem = poolbd_sem
            P.dma("pool", lambda e: e.dma_start(out=poolbd.ap, in_=poolbd_d[l].rearrange("c p n -> p c n")), pb_sem, writes=[poolbd])
            ys = []
            for cc in range(2):
                yap, yb = yring.next()
                ys.append((yap, yb))
                for t in range(4):
                    b = palloc()
                    for k in range(4):
                        tb = t * 4 + k
                        for gh in range(2):
                            wi = cc * 2 + gh
                            first = (tb == 0)
                            bm = mats[:, (11 if first else 3) + wi, :]
                            P.op("pe", lambda e, b=b, k=k, gh=gh, tb=tb, bm=bm, first=first, cc=cc: e.matmul(
                                ps[gh * 64:(gh + 1) * 64, b, k * 128:(k + 1) * 128],
                                utm[:, tb, cc * 128 + gh * 64: cc * 128 + (gh + 1) * 64], bm, start=True, stop=first),
                                reads=[qkb[tb], constb], writes=[pb[b]])
                            if not first:
                                P.op("pe", lambda e, b=b, k=k, gh=gh, tb=tb, wi=wi, cc=cc: e.matmul(
                                    ps[gh * 64:(gh + 1) * 64, b, k * 128:(k + 1) * 128],
                                    utm[:, tb - 1, cc * 128 + gh * 64: cc * 128 + (gh + 1) * 64], mats[:, 7 + wi, :],
                                    start=False, stop=True), reads=[qkb[tb - 1], constb], writes=[pb[b]])
                    pl = ering.next()
                    P.op("act", lambda e, pl=pl, b=b: e.activation(out=pl.ap, in_=ps[:, b, :], func=AF.Copy),
                         reads=[pb[b]], writes=[pl])
                    pfree(b)
                    b2 = palloc()
                    P.op("pe", lambda e, pl=pl, b2=b2, cc=cc: e.matmul(ps[:, b2, :], poolbd.ap[:, cc, :], pl.ap, start=True, stop=True),
                         reads=[pl, poolbd], writes=[pb[b2]])
                    P.op("dve", lambda e, b2=b2, t=t, yap=yap, cc=cc: e.tensor_scalar(
                        out=yap[:, tsl(t)], in0=ps[:, b2, :], scalar1=small[:, SP_PSC + l * 2 + cc: SP_PSC + l * 2 + cc + 1],
                        scalar2=None, op0=ALU.mult), reads=[pb[b2], constb], writes=[yb[t]])
                    pfree(b2)
            return ys

        deferred = []

        def run_deferred():
            while deferred:
                deferred.pop(0)()

        deferred_b = []

        def run_deferred_b():
            while deferred_b:
                deferred_b.pop(0)()

        def attn_stream(jobs, qk_fn, pv_fn, iter_end, la=4, G=2):
            n = len(jobs)
            st = [None] * n
            for i in range(min(la, n)):
                st[i] = qk_fn(jobs[i])
            i = 0
            cnt_a = None
            cnt_b = None
            while i < n:
                grp = list(range(i, min(i + G, n)))
                extra = [st[j][0] for j in grp[1:]]
                for gi, j in enumerate(grp):
                    pv_fn(jobs[j], st[j], extra if gi == 0 else [])
                for j in grp:
                    if j + la < n:
                        st[j + la] = qk_fn(jobs[j + la])
                if cnt_a is not None:
                    cnt_a -= 1
                    if cnt_a <= 0:
                        run_deferred()
                        cnt_a = None
                if cnt_b is not None:
                    cnt_b -= 1
                    if cnt_b <= 0:
                        run_deferred_b()
                        cnt_b = None
                for j in grp:
                    if j in iter_end:
                        run_deferred()
                        run_deferred_b()
                        iter_end[j]()
                        cnt_a = 1
                        cnt_b = 6
                i += G
            run_deferred()
            run_deferred_b()

        def diff_unit(l, h):
            qkT, vtm, qkb, vb, _ = project_unit(l, 256 + h * 384, 384, "D")
            yap, yb = yring.next()
            if dlevel <= 1:
                return (yap, yb)
            bO = [pfix(0), pfix(1)]
            bL = [pfix(2), pfix(3)]
            jobs = [(qi, kb, m) for qi in range(4) for kb in range(4 * (qi + 1)) for m in range(2)]

            def qk_fn(job):
                qi, kb, m = job
                r = kb - 4 * qi
                c0 = max(r, 0) * 128
                bS = palloc()
                qdeps = [qkb[kb]] + [qkb[qi * 4 + i] for i in range(c0 // 128, 4)]
                P.op("pe", lambda e: e.matmul(ps[:, bS, c0:512], qkT[m * 64:(m + 1) * 64, 1, kb * 128:(kb + 1) * 128],
                                              qkT[m * 64:(m + 1) * 64, 0, qi * 512 + c0:(qi + 1) * 512], start=True, stop=True),
                     reads=qdeps, writes=[pb[bS]])
                E = ering.next()
                P.op("act", lambda e: e.activation(out=E.ap[:, c0:512], in_=ps[:, bS, c0:512], func=AF.Exp, scale=0.125),
                     reads=[pb[bS]], writes=[E])
                pfree(bS)
                if r >= 0:
                    P.op("pool", lambda e: e.tensor_tensor(out=E.ap[:, c0:c0 + 128], in0=E.ap[:, c0:c0 + 128], in1=tri,
                                                           op=ALU.mult), reads=[E, constb], writes=[E])
                return (E, c0)

            def pv_fn(job, st, extra=()):
                qi, kb, m = job
                nkb = 4 * (qi + 1)
                E, c0 = st
                P.op("pe", lambda e: e.matmul(ps[:, bO[m], c0:512], vtm[:, kb, :], E.ap[:, c0:512],
                                              start=(kb == 0), stop=(kb == nkb - 1)), reads=[E, vb[kb]] + list(extra),
                     writes=[pb[bO[m]]])
                P.op("pe", lambda e: e.matmul(ps[:, bL[m], c0:512], ones, E.ap[:, c0:512],
                                              start=(kb == 0), stop=(kb == nkb - 1)), reads=[E, constb], writes=[pb[bL[m]]])

            def post(qi):
                A = tring.next()
                B = tring.next()
                LA = tring.next()
                LB = tring.next()
                for (T_, bo) in ((A, bO[0]), (B, bO[1])):
                    P.op("dve", lambda e, T_=T_, bo=bo: e.tensor_copy(out=T_.ap, in_=ps[:, bo, :]), reads=[pb[bo]], writes=[T_])
                for (T_, bl) in ((LA, bL[0]), (LB, bL[1])):
                    P.op("act", lambda e, T_=T_, bl=bl: e.activation(out=T_.ap, in_=ps[:, bl, :], func=AF.Ln), reads=[pb[bl]],
                         writes=[T_])

                dsq = dsqring.next()

                def part_a():
                    for T_ in (LA, LB):
                        P.op("act", lambda e, T_=T_: e.activation(out=T_.ap, in_=T_.ap, func=AF.Exp, scale=-1.0), reads=[T_],
                             writes=[T_])
                    for (T_, R_) in ((A, LA), (B, LB)):
                        P.op("dve", lambda e, T_=T_, R_=R_: e.tensor_tensor(out=T_.ap, in0=T_.ap, in1=R_.ap, op=ALU.mult),
                             reads=[T_, R_], writes=[T_])
                    P.op("dve", lambda e: e.scalar_tensor_tensor(out=A.ap, in0=B.ap, scalar=neglam[:, l:l + 1], in1=A.ap,
                                                                 op0=ALU.mult, op1=ALU.add), reads=[A, B, derb], writes=[A])
                    P.op("pool", lambda e: e.tensor_tensor(out=dsq.ap, in0=A.ap, in1=A.ap, op=ALU.mult), reads=[A], writes=[dsq])

                def part_b():
                    bs = palloc()
                    P.op("pe", lambda e: e.matmul(ps[:, bs, :], ones, dsq.ap, start=True, stop=True), reads=[dsq, constb],
                         writes=[pb[bs]])
                    P.op("act", lambda e: e.activation(out=LB.ap, in_=ps[:, bs, :], func=AF.Ln, scale=1.0 / 128.0, bias=EPS),
                         reads=[pb[bs]], writes=[LB])
                    pfree(bs)
                    P.op("act", lambda e: e.activation(out=LB.ap, in_=LB.ap, func=AF.Exp, scale=-0.5), reads=[LB], writes=[LB])
                    P.op("dve", lambda e: e.scalar_tensor_tensor(out=yap[:, tsl(qi)], in0=A.ap, scalar=gsub[:, l:l + 1], in1=LB.ap,
                                                                 op0=ALU.mult, op1=ALU.mult), reads=[A, LB, derb], writes=[yb[qi]])
                deferred.append(part_a)
                deferred_b.append(part_b)

            iter_end = {}
            idx = -1
            for qi in range(4):
                idx += 2 * 4 * (qi + 1)
                iter_end[idx] = (lambda qi=qi: post(qi))
            attn_stream(jobs, qk_fn, pv_fn, iter_end, la=4, G=2)
            for b_ in bO + bL:
                pfree(b_)
            return (yap, yb)

        def moba_unit(l, p_):
            qkT, vtm, qkb, vb, _ = project_unit(l, 256 + 4 * 384 + p_ * 384, 384, "M")
            yap, yb = yring.next()
            P.op("dve", lambda e: e.tensor_reduce(out=km32, in_=qkT[:, 1, :].rearrange("p (j t) -> p j t", j=8), axis=AX.X,
                                                  op=ALU.add), reads=qkb, writes=[gmb])
            P.op("dve", lambda e: e.tensor_scalar(out=kmbf, in0=km32, scalar1=1.0 / 256.0, scalar2=None, op0=ALU.mult),
                 reads=[gmb], writes=[gmb])
            bg = palloc()
            for hh in range(2):
                for qb in range(8, 16):
                    idx = hh * 8 + (qb - 8)
                    P.op("pe", lambda e, hh=hh, qb=qb, idx=idx: e.matmul(
                        ps[:, bg, idx * 8:(idx + 1) * 8], qkT[hh * 64:(hh + 1) * 64, 0, qb * 128:(qb + 1) * 128],
                        kmbf[hh * 64:(hh + 1) * 64, :], start=True, stop=True), reads=[qkb[qb], gmb], writes=[pb[bg]])
            nmb = [[Buf() for _ in range(8)] for _ in range(2)]
            nmt = tring.next()
            nmv = nmt.ap[:, 0:64].bitcast(BF16).rearrange("p (i j) -> p i j", i=16)
            for hh in range(2):
                P.op("dve", lambda e: e.memset(gm, -1e30), writes=[gmb])
                for qb in range(8, 16):
                    idx = hh * 8 + (qb - 8)
                    own = qb // 2
                    P.op("dve", lambda e, idx=idx, own=own: e.tensor_copy(out=gm[:, 0:own], in_=ps[:, bg, idx * 8: idx * 8 + own]),
                         reads=[pb[bg], gmb], writes=[gmb])
                    P.op("dve", lambda e: e.max(out=top8, in_=gm), reads=[gmb], writes=[gmb])
                    P.op("dve", lambda e, idx=idx, own=own: e.tensor_copy(out=nmv[:, idx, :], in_=nmall[:, own, :]),
                         reads=[constb, gmb], writes=[nmt])
                    P.op("dve", lambda e, idx=idx, own=own: e.tensor_scalar(
                        out=nmv[:, idx, 0:own], in0=gm[:, 0:own], scalar1=top8[:, 2:3], scalar2=NEG, op0=ALU.is_lt, op1=ALU.mult),
                        reads=[gmb, nmt], writes=[nmt])
            pfree(bg)
            for hh in range(2):
                for half in range(2):
                    bt = palloc()
                    for i in range(4):
                        idx = hh * 8 + half * 4 + i
                        P.op("pe", lambda e, i=i, idx=idx, bt=bt: e.matmul(ps[0:8, bt, i * 128:(i + 1) * 128], nmv[:, idx, :], ident,
                                                                          start=True, stop=True),
                             reads=[nmt, constb], writes=[pb[bt]])
                    P.op("act", lambda e, hh=hh, half=half, bt=bt: e.activation(
                        out=negT[0:8, hh, 1024 + half * 512: 1024 + (half + 1) * 512], in_=ps[0:8, bt, :], func=AF.Copy),
                        reads=[pb[bt]], writes=[negTb])
                    pfree(bt)
            bO = pfix(0)
            bL = pfix(1)
            jobs = [(qi, kb, hh) for qi in range(4) for hh in range(2) for kb in range(4 * (qi + 1))]

            def qk_fn(job):
                qi, kb, hh = job
                r = kb - 4 * qi
                c0 = max(r, 0) * 128
                need_mask = (qi >= 2) and (r < 2)
                bS = palloc()
                qdeps = [qkb[kb]] + [qkb[qi * 4 + i] for i in range(c0 // 128, 4)]
                P.op("pe", lambda e: e.matmul(ps[:, bS, c0:512], qkT[hh * 64:(hh + 1) * 64, 1, kb * 128:(kb + 1) * 128],
                                              qkT[hh * 64:(hh + 1) * 64, 0, qi * 512 + c0:(qi + 1) * 512],
                                              start=True, stop=(not need_mask)), reads=qdeps, writes=[pb[bS]])
                if need_mask:
                    P.op("pe", lambda e: e.matmul(ps[:, bS, c0:512], oh[0:8, kb // 2, :],
                                                  negT[0:8, hh, qi * 512 + c0:(qi + 1) * 512], start=False, stop=True),
                         reads=[negTb, constb], writes=[pb[bS]])
                E = ering.next()
                P.op("act", lambda e: e.activation(out=E.ap[:, c0:512], in_=ps[:, bS, c0:512], func=AF.Exp, scale=0.125),
                     reads=[pb[bS]], writes=[E])
                pfree(bS)
                if r >= 0:
                    P.op("pool", lambda e: e.tensor_tensor(out=E.ap[:, c0:c0 + 128], in0=E.ap[:, c0:c0 + 128], in1=tri,
                                                           op=ALU.mult), reads=[E, constb], writes=[E])
                return (E, c0)

            def pv_fn(job, st, extra=()):
                qi, kb, hh = job
                nkb = 4 * (qi + 1)
                E, c0 = st
                P.op("pe", lambda e: e.matmul(ps[hh * 64:(hh + 1) * 64, bO, c0:512], vtm[:, kb, hh * 64:(hh + 1) * 64],
                                              E.ap[:, c0:512], start=(kb == 0), stop=(kb == nkb - 1)),
                     reads=[E, vb[kb]] + list(extra), writes=[pb[bO]])
                P.op("pe", lambda e: e.matmul(ps[hh * 64:(hh + 1) * 64, bL, c0:512], ones[:, 0:64], E.ap[:, c0:512],
                                              start=(kb == 0), stop=(kb == nkb - 1)), reads=[E, constb], writes=[pb[bL]])

            def post(qi):
                R_ = tring.next()
                OA = tring.next()
                P.op("dve", lambda e: e.tensor_copy(out=OA.ap, in_=ps[:, bO, :]), reads=[pb[bO]], writes=[OA])
                P.op("act", lambda e: e.activation(out=R_.ap, in_=ps[:, bL, :], func=AF.Ln), reads=[pb[bL]], writes=[R_])

                def part2():
                    P.op("act", lambda e: e.activation(out=R_.ap, in_=R_.ap, func=AF.Exp, scale=-1.0), reads=[R_], writes=[R_])
                    P.op("dve", lambda e: e.tensor_tensor(out=yap[:, tsl(qi)], in0=OA.ap, in1=R_.ap, op=ALU.mult),
                         reads=[OA, R_], writes=[yb[qi]])
                deferred.append(part2)

            iter_end = {}
            idx = -1
            for qi in range(4):
                idx += 2 * 4 * (qi + 1)
                iter_end[idx] = (lambda qi=qi: post(qi))
            attn_stream(jobs, qk_fn, pv_fn, iter_end, la=4, G=2)
            pfree(bO)
            pfree(bL)
            return (yap, yb)

        def mix(l):
            norm_to_xn(l * 3 + 1)
            if "P" in parts:
                ya = pool_unit(l)
                wout_pair(l, 0, ya[0], ya[1])
            if "D" in parts:
                d0 = diff_unit(l, 0)
                if nd <= 1:
                    return
                d1 = diff_unit(l, 1)
                if dlevel > 2:
                    wout_pair(l, 2, d0, d1)
                d2 = diff_unit(l, 2)
                d3 = diff_unit(l, 3)
                if dlevel > 2:
                    wout_pair(l, 4, d2, d3)
            if "M" in parts:
                m0 = moba_unit(l, 0)
                m1 = moba_unit(l, 1)
                wout_pair(l, 6, m0, m1)

        xsems = [newsem() for _ in range(8)]
        stores = []
        for seq in range(nseq):
            for c in range(8):
                P.dma("sp", lambda e, c=c, seq=seq: e.dma_start(out=xT[:, c, :], in_=xT_d[seq][c * 128:(c + 1) * 128, :]),
                      xsems[c], writes=xb[c])
            done = False
            for l in range(depth):
                if not skip_ffn:
                    ffn(l, 0)
                if stop == ("ffn1", l):
                    done = True
                    break
                bar = phase_barrier()
                for e_ in ("pe", "act", "dve", "pool"):
                    P.op(e_, (lambda e: e.nop()) if e_ != "pe" else (lambda e: e.nop()), deps=bar)
                mix(l)
                if stop == ("mix", l):
                    done = True
                    break
                bar = phase_barrier()
                for e_ in ("pe", "act", "dve", "pool"):
                    P.op(e_, lambda e: e.nop(), deps=bar)
                ffn(l, 1)
                if stop == ("ffn2", l):
                    done = True
                    break
            if not done:
                norm_final()
            for c in range(8):
                stores.append(P.dma("sp", lambda e, c=c, seq=seq: e.dma_start(out=out_d[seq][c * 128:(c + 1) * 128, :],
                                                                             in_=xT[:, c, :]), xsems[c], reads=xb[c]))
        P.op("sp", lambda e: e.nop(), deps=stores)
        P.emit(block, esems)
    return nc


_CACHE = {}


def _prep_shared(inp, depth):
    cs, mats, oh, negc, nmi = _const_tables()
    perm = _mix_perm()
    shared = {
        "ffn1_w_in": np.ascontiguousarray(inp["ffn1_w_in"][:depth], np.float32),
        "ffn2_w_in": np.ascontiguousarray(inp["ffn2_w_in"][:depth], np.float32),
        "ffn1_w_out": np.ascontiguousarray(inp["ffn1_w_out"][:depth], np.float32),
        "ffn2_w_out": np.ascontiguousarray(inp["ffn2_w_out"][:depth], np.float32),
        "mixw": np.ascontiguousarray(inp["mix_w_in"][:depth][:, :, perm], np.float32),
        "mixo": np.ascontiguousarray(inp["mix_w_out"][:depth], np.float32),
        "poolbd": _pool_bd(np.asarray(inp["pool_w"], np.float32), depth),
        "smallp": _small_params(inp, depth),
        "c_cs": cs, "c_mats": mats, "c_oh": oh, "c_neg": negc, "c_nmi": nmi,
    }
    return shared


def kernel(**inputs):
    inp = {k: np.asarray(v) for k, v in inputs.items()}
    x = inp["x"].astype(np.float32, copy=False)
    ncores = 8
    nseq = x.shape[0] // ncores
    if "nc" not in _CACHE:
        _CACHE["nc"] = build_program(nseq=nseq, depth=DEPTH)
    nc = _CACHE["nc"]
    shared = _prep_shared(inp, DEPTH)
    in_maps = []
    for i in range(ncores):
        xs = x[i * nseq:(i + 1) * nseq]
        m = dict(shared)
        m["xT"] = np.ascontiguousarray(xs.transpose(0, 2, 1))
        in_maps.append(m)
    res = run_bass_kernel_spmd(nc, in_maps, core_ids=list(range(ncores)))
    outs = [np.asarray(r["outT"]).transpose(0, 2, 1) for r in res.results]
    return np.ascontiguousarray(np.concatenate(outs, axis=0), dtype=np.float32)
```
